# Optimizing a Trainium2 kernel written in Bass

```python
import math
import jax, jax.numpy as jnp
from jax import lax
import numpy as np

D_MODEL = 1024
BATCH = 8
SEQ = 2048
DEPTH = 4
DEC_BATCH = 128
DEC_SEQ = 1
PAST_LEN = 16384
PAGE_SIZE = 128

D_MIX = D_MODEL
POOL_WIDTH = D_MIX // 4
POOL_WINDOWS = (2, 4, 8, 16)
POOL_GROUPS = len(POOL_WINDOWS)
POOL_GROUP_DIM = POOL_WIDTH // POOL_GROUPS
POOL_BUF = max(POOL_WINDOWS) - 1
DN_WIDTH = D_MIX // 2
DN_HEAD_DIM = 128
DN_HEADS = DN_WIDTH // DN_HEAD_DIM
DN_CONV = 4
DN_CHUNK = 64
CONF_WIDTH = D_MIX - POOL_WIDTH - DN_WIDTH
CONF_HEADS = 4
CONF_HEAD_DIM = CONF_WIDTH // CONF_HEADS
CONF_WIDTH_K = 31
D_FF = 4 * D_MODEL
EPS = 1e-6
OFF_POOL = 0
OFF_QKV = OFF_POOL + POOL_WIDTH
OFF_Z = OFF_QKV + 3 * DN_WIDTH
OFF_B = OFF_Z + DN_WIDTH
OFF_A = OFF_B + DN_HEADS
OFF_GLU = OFF_A + DN_HEADS
N_IN = OFF_GLU + 2 * CONF_WIDTH

kernel_name = 'hybrid_pool_gdn_conformer_decoder_step'


def _rmsnorm(x, g):
    x32 = x.astype(jnp.float32)
    y = x32 * lax.rsqrt(jnp.mean(x32 * x32, axis=-1, keepdims=True) + EPS)
    return (y * g.astype(jnp.float32)).astype(x.dtype)


def _l2norm(x):
    return x * lax.rsqrt(jnp.sum(x * x, axis=-1, keepdims=True) + EPS)


def _causal_dwconv(x_ext, w):
    ch = w.shape[1]
    return lax.conv_general_dilated(x_ext, w[:, None, :].astype(x_ext.dtype), window_strides=(1,),
                                    padding='VALID', dimension_numbers=('NWC', 'WIO', 'NWC'),
                                    feature_group_count=ch)


def _pool_mixer(u_ext, pos0, pool_w, pool_scale):
    B, L, _ = u_ext.shape
    T = L - POOL_BUF
    u32 = u_ext.astype(jnp.float32)
    cs = jnp.pad(jnp.cumsum(u32, axis=1), ((0, 0), (1, 0), (0, 0)))
    pos = pos0 + jnp.arange(T, dtype=jnp.int32)
    means = []
    for gi, w in enumerate(POOL_WINDOWS):
        lo, hi = gi * POOL_GROUP_DIM, (gi + 1) * POOL_GROUP_DIM
        s = cs[:, POOL_BUF + 1:, lo:hi] - cs[:, POOL_BUF + 1 - w:POOL_BUF + 1 - w + T, lo:hi]
        cnt = jnp.minimum(pos + 1, w).astype(jnp.float32)[None, :, None]
        means.append(s / cnt)
    d = (jnp.concatenate(means, axis=-1) - u32[:, POOL_BUF:]).reshape(B, T, POOL_GROUPS, POOL_GROUP_DIM)
    y = jnp.einsum('btgc,gcd->btgd', d, pool_w.astype(jnp.float32)).reshape(B, T, POOL_WIDTH)
    return (y * pool_scale.astype(jnp.float32)).astype(u_ext.dtype)


def _gated_delta(q, k, v, beta, g, s0):
    B, T, H, DK = q.shape
    DV = v.shape[-1]
    C = min(DN_CHUNK, T)
    n = -(-T // C)
    pad = n * C - T
    if pad:
        q, k, v = [jnp.pad(a, ((0, 0), (0, pad), (0, 0), (0, 0))) for a in (q, k, v)]
        beta, g = [jnp.pad(a, ((0, 0), (0, pad), (0, 0))) for a in (beta, g)]

    def chunks(a):
        return jnp.moveaxis(a.reshape((B, n, C) + a.shape[2:]), 3, 1)

    q, k, v, beta, g = [chunks(a) for a in (q, k, v, beta, g)]
    G = jnp.cumsum(g, axis=-1)
    idx = jnp.arange(C)
    incl = idx[:, None] >= idx[None, :]
    strict = idx[:, None] > idx[None, :]
    diff = G[..., :, None] - G[..., None, :]
    decay = jnp.where(incl, jnp.exp(jnp.where(incl, diff, 0.0)), 0.0)
    kk = jnp.einsum('bhncd,bhnsd->bhncs', k, k)
    tmat = jnp.where(strict, beta[..., :, None] * kk * decay, 0.0) + jnp.eye(C, dtype=q.dtype)
    rhs = jnp.concatenate([v * beta[..., None], k * (beta * jnp.exp(G))[..., None]], axis=-1)
    sol = lax.linalg.triangular_solve(tmat, rhs, left_side=True, lower=True, unit_diagonal=True)
    u_base, w_cum = sol[..., :DV], sol[..., DV:]
    qk = jnp.einsum('bhncd,bhnsd->bhncs', q, k) * decay
    q_g = q * jnp.exp(G)[..., None]
    k_tail = k * jnp.exp(G[..., -1:] - G)[..., None]
    g_last = jnp.exp(G[..., -1])

    def step(S, inp):
        u_b, w_c, qk_c, qg_c, kt_c, gl_c = inp
        u = u_b - jnp.einsum('bhcd,bhdv->bhcv', w_c, S)
        o = jnp.einsum('bhcd,bhdv->bhcv', qg_c, S) + jnp.einsum('bhcs,bhsv->bhcv', qk_c, u)
        S = S * gl_c[..., None, None] + jnp.einsum('bhcd,bhcv->bhdv', kt_c, u)
        return S, o

    xs = tuple(jnp.moveaxis(a, 2, 0) for a in (u_base, w_cum, qk, q_g, k_tail, g_last))
    s_new, o = lax.scan(step, s0, xs)
    o = o.transpose(1, 0, 3, 2, 4).reshape(B, n * C, H, DV)[:, :T]
    return o, s_new


def _layer(x, c, pos0, pool_buf, conv_buf, s0, conf_buf, lp):
    (w_ada, b_ada, g_norm1, g_norm2, w_in, pool_w, pool_scale, qkv_conv_w, a_log, dt_bias,
     dn_norm_g, conf_dw_w, conf_dw_b, conf_ln_g, conf_ln_b, conf_pw_w, w_out, w_ff1, w_ff2) = lp
    f32 = jnp.float32
    B, T, _ = x.shape
    mod = jax.nn.silu(c) @ w_ada + b_ada
    shift1, scale1, gate1, shift2, scale2, gate2 = [m[:, None, :] for m in jnp.split(mod, 6, axis=-1)]
    h = _rmsnorm(x, g_norm1) * (1 + scale1) + shift1
    proj = h @ w_in

    pool_ext = jnp.concatenate([pool_buf, proj[..., OFF_POOL:OFF_QKV]], axis=1)
    y_a = _pool_mixer(pool_ext, pos0, pool_w, pool_scale)
    new_pool = pool_ext[:, -POOL_BUF:]

    qkv_ext = jnp.concatenate([conv_buf, proj[..., OFF_QKV:OFF_Z]], axis=1)
    qkv = jax.nn.silu(_causal_dwconv(qkv_ext, qkv_conv_w)).astype(f32)
    new_conv = qkv_ext[:, -(DN_CONV - 1):]
    q, k, v = [a.reshape(B, T, DN_HEADS, DN_HEAD_DIM) for a in jnp.split(qkv, 3, axis=-1)]
    q = _l2norm(q) * (DN_HEAD_DIM ** -0.5)
    k = _l2norm(k)
    beta = jax.nn.sigmoid(proj[..., OFF_B:OFF_A].astype(f32))
    g = -jnp.exp(a_log.astype(f32)) * jax.nn.softplus(proj[..., OFF_A:OFF_GLU].astype(f32) + dt_bias.astype(f32))
    o, s_new = _gated_delta(q, k, v, beta, g, s0.astype(f32))
    z = proj[..., OFF_Z:OFF_B].astype(f32).reshape(B, T, DN_HEADS, DN_HEAD_DIM)
    o = o * lax.rsqrt(jnp.mean(o * o, axis=-1, keepdims=True) + EPS) * dn_norm_g.astype(f32) * jax.nn.silu(z)
    y_b = o.reshape(B, T, DN_WIDTH).astype(x.dtype)

    glu = proj[..., OFF_GLU:OFF_GLU + CONF_WIDTH] * jax.nn.sigmoid(proj[..., OFF_GLU + CONF_WIDTH:])
    conf_ext = jnp.concatenate([conf_buf, glu], axis=1)
    new_conf = conf_ext[:, -(CONF_WIDTH_K - 1):]
    dc = (_causal_dwconv(conf_ext, conf_dw_w) + conf_dw_b).astype(f32).reshape(B, T, CONF_HEADS, CONF_HEAD_DIM)
    mu = jnp.mean(dc, axis=-1, keepdims=True)
    var = jnp.mean(jnp.square(dc - mu), axis=-1, keepdims=True)
    dn = ((dc - mu) * lax.rsqrt(var + EPS)).reshape(B, T, CONF_WIDTH) * conf_ln_g.astype(f32) + conf_ln_b.astype(f32)
    y_c = jax.nn.silu(dn).astype(x.dtype) @ conf_pw_w

    mix = jnp.concatenate([y_a, y_b, y_c], axis=-1) @ w_out
    x = x + gate1 * mix

    hf = _rmsnorm(x, g_norm2) * (1 + scale2) + shift2
    a = jax.nn.relu(hf @ w_ff1)
    x = x + gate2 * ((a * a) @ w_ff2)
    return x, new_pool, new_conv, s_new, new_conf


def setup_inputs(seed: int = 0) -> dict:
    key = jax.random.key(seed)
    ks = iter(jax.random.split(key, 40))

    def nrm(shape, std):
        return jax.random.normal(next(ks), shape, jnp.float32) * std

    def gain(shape):
        return 1.0 + nrm(shape, 0.02)

    x_prompt = nrm((BATCH, SEQ, D_MODEL), 1.0)
    x_sample = nrm((DEC_BATCH, DEC_SEQ, D_MODEL), 1.0)
    state_pool = nrm((DEPTH, DEC_BATCH, POOL_BUF, POOL_WIDTH), 1.0)
    state_qkv_conv = nrm((DEPTH, DEC_BATCH, DN_CONV - 1, 3 * DN_WIDTH), 1.0)
    state_delta = nrm((DEPTH, DEC_BATCH, DN_HEADS, DN_HEAD_DIM, DN_HEAD_DIM), 0.1)
    state_conv = nrm((DEPTH, DEC_BATCH, CONF_WIDTH_K - 1, CONF_WIDTH), 0.5)
    c_prompt = nrm((BATCH, D_MODEL), 1.0)
    c_sample = nrm((DEC_BATCH, D_MODEL), 1.0)
    a_log = jnp.log(jax.random.uniform(next(ks), (DEPTH, DN_HEADS), jnp.float32, 1.0, 16.0))
    dt = jnp.exp(jax.random.uniform(next(ks), (DEPTH, DN_HEADS), jnp.float32, math.log(1e-3), math.log(1e-1)))
    dt_bias = jnp.log(jnp.expm1(dt))
    return {
        'x_prompt': x_prompt, 'x_sample': x_sample,
        'state_pool': state_pool, 'state_qkv_conv': state_qkv_conv,
        'state_delta': state_delta, 'state_conv': state_conv,
        'c_prompt': c_prompt, 'c_sample': c_sample,
        'w_ada': nrm((DEPTH, D_MODEL, 6 * D_MODEL), 0.5 * D_MODEL ** -0.5),
        'b_ada': nrm((DEPTH, 6 * D_MODEL), 0.01),
        'g_norm1': gain((DEPTH, D_MODEL)), 'g_norm2': gain((DEPTH, D_MODEL)),
        'w_in': nrm((DEPTH, D_MODEL, N_IN), D_MODEL ** -0.5),
        'pool_w': nrm((DEPTH, POOL_GROUPS, POOL_GROUP_DIM, POOL_GROUP_DIM), POOL_GROUP_DIM ** -0.5),
        'pool_scale': gain((DEPTH, POOL_WIDTH)),
        'qkv_conv_w': nrm((DEPTH, DN_CONV, 3 * DN_WIDTH), DN_CONV ** -0.5),
        'a_log': a_log, 'dt_bias': dt_bias,
        'dn_norm_g': gain((DEPTH, DN_HEAD_DIM)),
        'conf_dw_w': nrm((DEPTH, CONF_WIDTH_K, CONF_WIDTH), CONF_WIDTH_K ** -0.5),
        'conf_dw_b': nrm((DEPTH, CONF_WIDTH), 0.02),
        'conf_ln_g': gain((DEPTH, CONF_WIDTH)), 'conf_ln_b': nrm((DEPTH, CONF_WIDTH), 0.02),
        'conf_pw_w': nrm((DEPTH, CONF_WIDTH, CONF_WIDTH), CONF_WIDTH ** -0.5),
        'w_out': nrm((DEPTH, D_MIX, D_MODEL), D_MIX ** -0.5),
        'w_ff1': nrm((DEPTH, D_MODEL, D_FF), D_MODEL ** -0.5),
        'w_ff2': nrm((DEPTH, D_FF, D_MODEL), D_FF ** -0.5),
        'g_final': gain((D_MODEL,)),
    }


def reference(x_prompt, x_sample, state_pool, state_qkv_conv, state_delta, state_conv, c_prompt, c_sample,
              w_ada, b_ada, g_norm1, g_norm2, w_in, pool_w, pool_scale, qkv_conv_w, a_log, dt_bias,
              dn_norm_g, conf_dw_w, conf_dw_b, conf_ln_g, conf_ln_b, conf_pw_w, w_out, w_ff1, w_ff2, g_final):
    bp = x_prompt.shape[0]
    dt_ = x_prompt.dtype
    zero_pool = jnp.zeros((bp, POOL_BUF, POOL_WIDTH), dt_)
    zero_conv = jnp.zeros((bp, DN_CONV - 1, 3 * DN_WIDTH), dt_)
    zero_s = jnp.zeros((bp, DN_HEADS, DN_HEAD_DIM, DN_HEAD_DIM), jnp.float32)
    zero_conf = jnp.zeros((bp, CONF_WIDTH_K - 1, CONF_WIDTH), dt_)
    xp, xs = x_prompt, x_sample
    pp, pc, ps, pf = [], [], [], []
    sp, sc, ss, sf = [], [], [], []
    for l in range(DEPTH):
        lp = (w_ada[l], b_ada[l], g_norm1[l], g_norm2[l], w_in[l], pool_w[l], pool_scale[l], qkv_conv_w[l],
              a_log[l], dt_bias[l], dn_norm_g[l], conf_dw_w[l], conf_dw_b[l], conf_ln_g[l], conf_ln_b[l],
              conf_pw_w[l], w_out[l], w_ff1[l], w_ff2[l])
        xp, n_pool, n_conv, n_s, n_conf = _layer(xp, c_prompt, 0, zero_pool, zero_conv, zero_s, zero_conf, lp)
        pp.append(n_pool); pc.append(n_conv); ps.append(n_s); pf.append(n_conf)
        xs, n_pool, n_conv, n_s, n_conf = _layer(xs, c_sample, PAST_LEN, state_pool[l], state_qkv_conv[l],
                                                 state_delta[l], state_conv[l], lp)
        sp.append(n_pool); sc.append(n_conv); ss.append(n_s); sf.append(n_conf)
    y_prompt = _rmsnorm(xp, g_final)
    y_sample = _rmsnorm(xs, g_final)
    return (y_prompt, y_sample,
            jnp.stack(pp), jnp.stack(sp),
            jnp.stack(pc), jnp.stack(sc),
            jnp.stack(ps), jnp.stack(ss),
            jnp.stack(pf), jnp.stack(sf))
```

```python
import numpy as np
from contextlib import ExitStack
import concourse.bass as bass
import concourse.mybir as mybir
from concourse.bass_utils import run_bass_kernel_spmd

F32 = mybir.dt.float32
BF16 = mybir.dt.bfloat16
AF = mybir.ActivationFunctionType
ALU = mybir.AluOpType
AX = mybir.AxisListType

NCORES = 8
L = 4
D = 1024
KC = 8
T = 2048
NS = 16
TT = 256
NTILE = T // TT
NIN = 2824
DFF = 4096
EPS = 1e-6
FUSE_WAIT = True
USE_LS = True
WINDOWS = (2, 4, 8, 16)

P_G1, P_G2, P_BADA, P_PSC, P_QW, P_CW, P_CB, P_LG, P_LB, P_DG, P_ALOG, P_DTB, NPAR = \
    0, 8, 16, 64, 66, 114, 176, 178, 180, 182, 183, 184, 192
C_ID, C_U, C_MS, C_TS, C_ONES, C_BLK, C_FIX, C_INVW, C_M03, C_M47, C_EPS, C_ONE, C_BD, NCST = \
    0, 128, 256, 384, 512, 640, 768, 798, 800, 801, 802, 803, 804, 932


class _Op:
    __slots__ = ("eng", "call", "reads", "writes", "dma", "waits", "sig", "snap", "signaled", "barrier")


class Sched:
    ENGS = ("pe", "dve", "act", "pool", "sp")
    NDMA = 24

    def __init__(self):
        self.ops = []
        self.efree = {}
        self.kfin = {}
        self.stage_fin = 0.0

    def add(self, eng, meth, reads, writes, args, kw, dma=False):
        o = _Op()
        ex = [k for k in reads if isinstance(k, tuple) and k[0] == "psbank"]
        if ex:
            reads = [k for k in reads if not (isinstance(k, tuple) and k[0] == "psbank")]
            writes = list(writes) + [k for k in ex if k not in writes]
        o.eng, o.call, o.reads, o.writes, o.dma = eng, (meth, args, kw), tuple(reads), tuple(writes), dma
        o.waits, o.sig, o.snap, o.signaled, o.barrier = {}, None, None, False, False
        self.ops.append(o)
        try:
            if dma:
                cost = 2000.0
            else:
                outap = kw.get("out", kw.get("ap", args[0] if args else None))
                n = 1
                for d in outap.shape[1:]:
                    n *= int(d)
                if eng == "pe":
                    if meth == "matmul":
                        cost = (4.0 if kw["lhsT"].dtype == F32 else 1.0) * max(n, 64) / 2.4 + 40.0
                    else:
                        cost = 120.0
                elif eng == "dve":
                    cost = (n + 151) / 0.96
                elif eng == "act":
                    cost = (n + 400) / 1.4
                else:
                    cost = (n + 250) / 0.7
            t0 = self.efree.get(eng, 0.0)
            for k in o.reads + o.writes:
                t = self.kfin.get(k)
                if t is not None and t + 250.0 > t0:
                    t0 = t + 250.0
            t1 = t0 + cost
            self.efree[eng] = t1
            for k in o.writes:
                self.kfin[k] = t1
            self.stage_fin = max(self.stage_fin, t1)
        except Exception:
            pass
        return o

    def barrier(self, dummy_ap):
        o = self.add("dve", "memset", [], [], (), dict(ap=dummy_ap, constant=0.0))
        o.barrier = True

    def analyze(self):
        clock = {e: {} for e in self.ENGS}
        last_w = {}
        readers = {}
        cnt = {e: 0 for e in self.ENGS}
        dma_issued = [0] * self.NDMA
        dma_last = [None] * self.NDMA
        rrq = {}
        last_eng = {}
        pending = {}
        for op in self.ops:
            E = op.eng
            ck = clock[E]
            deps = []
            if op.barrier:
                deps.extend(last_eng.values())
                deps.extend(d for d in dma_last if d is not None)
                pending = {e: op for e in self.ENGS if e != "dve"}
            elif E in pending:
                deps.append(pending.pop(E))
            for r in op.reads:
                p = last_w.get(r)
                if p is not None:
                    deps.append(p)
            for r in op.writes:
                p = last_w.get(r)
                if p is not None:
                    if not (p.eng == E and isinstance(r, tuple) and r[0] == "psbank"):
                        deps.append(p)
                rd = readers.get(r)
                if rd:
                    deps.extend(rd[0].values())
                    deps.extend(rd[1])
            waits = {}
            merges = []
            for p in deps:
                key, val = p.sig
                if key == "pe" and E == "pe" and not op.dma:
                    continue
                if ck.get(key, 0) >= val or waits.get(key, 0) >= val:
                    continue
                waits[key] = val
                p.signaled = True
                merges.append(p.snap)
            if op.dma:
                s = None
                half = self.NDMA // 2
                base = 0 if E == "sp" else half
                r0 = rrq.get(E, 0)
                for i in range(half):
                    c = base + (r0 + i) % half
                    if max(ck.get(("d", c), 0), waits.get(("d", c), 0)) >= dma_issued[c]:
                        s = c
                        break
                if s is None:
                    s = base + r0
                    waits[("d", s)] = dma_issued[s]
                    merges.append(dma_last[s].snap)
                rrq[E] = (s - base + 1) % half
            for k, v in waits.items():
                if ck.get(k, 0) < v:
                    ck[k] = v
            for sn in merges:
                for k, v in sn.items():
                    if ck.get(k, 0) < v:
                        ck[k] = v
            op.waits = waits
            if op.dma:
                dma_issued[s] += 1
                op.sig = (("d", s), dma_issued[s])
                dma_last[s] = op
            else:
                cnt[E] += 1
                op.sig = (E, cnt[E])
            sn = dict(ck)
            sn[op.sig[0]] = op.sig[1]
            op.snap = sn
            if not op.dma:
                last_eng[E] = op
            for r in op.writes:
                last_w[r] = op
                readers[r] = ({}, [])
            for r in op.reads:
                rd = readers.get(r)
                if rd is None:
                    rd = readers[r] = ({}, [])
                if op.dma:
                    rd[1].append(op)
                else:
                    rd[0][E] = op
        self.rank = {}
        c2 = {e: 0 for e in self.ENGS}
        for op in self.ops:
            if not op.dma and op.signaled:
                c2[op.eng] += 1
                self.rank[(op.eng, op.sig[1])] = c2[op.eng]
        self.dma_final = dma_issued
        for op in self.ops:
            op.snap = None

    def emit(self, nc, es):
        sems = {}
        for e in ("pe", "dve", "act", "pool"):
            sems[e] = es.enter_context(nc.semaphore("s_" + e))
        for i in range(self.NDMA):
            sems[("d", i)] = es.enter_context(nc.semaphore("s_d%d" % i))
        per = {e: [o for o in self.ops if o.eng == e] for e in self.ENGS}
        block = es.enter_context(nc.Block())

        def run(E, e):
            for op in per[E]:
                wl = [(sems[key], (16 * val if isinstance(key, tuple) else self.rank[(key, val)])) for key, val in op.waits.items()]
                fused = None
                if wl and not op.dma and FUSE_WAIT:
                    fused = wl.pop()
                for sm, v in wl:
                    e.wait_ge(sm, v)
                meth, args, kw = op.call
                ins = getattr(e, meth)(*args, **kw)
                if fused is not None:
                    ins._wait_ge(fused[0], fused[1])
                if op.dma:
                    ins.then_inc(sems[op.sig[0]], 16)
                elif op.signaled:
                    ins.then_inc(sems[op.sig[0]], 1)
            if E == "sp":
                for i in range(self.NDMA):
                    if self.dma_final[i]:
                        e.wait_ge(sems[("d", i)], 16 * self.dma_final[i])

        @block.tensor
        def _(e):
            run("pe", e)

        @block.vector
        def _(e):
            run("dve", e)

        @block.scalar
        def _(e):
            run("act", e)

        @block.gpsimd
        def _(e):
            run("pool", e)

        @block.sync
        def _(e):
            run("sp", e)


def build_program():
    nc = bass.Bass("TRN2", target_bir_lowering=False)
    S = Sched()
    es = ExitStack()

    def din(name, shape):
        return nc.dram_tensor(name, list(shape), F32, kind="ExternalInput").ap()

    def dout(name, shape):
        return nc.dram_tensor(name, list(shape), F32, kind="ExternalOutput").ap()

    xT_d = din("xT", [D, T])
    xsT_d = din("xsT", [D, NS])
    cT_d = din("cT", [D, 17])
    pp_d = din("pp", [128, L, NPAR])
    gf_d = din("gf", [128, KC])
    cst_d = din("cst", [128, NCST])
    w_ada_d = din("w_ada", [L, D, 6 * D])
    w_in_d = din("w_in", [L, D, NIN])
    w_out_d = din("w_out", [L, D, D])
    w_ff1_d = din("w_ff1", [L, D, DFF])
    w_ff2_d = din("w_ff2", [L, DFF, D])
    pool_w_d = din("pool_w", [L, 4, 64, 64])
    pw_d = din("conf_pw_w", [L, 256, 256])
    st_pool_d = din("st_pool", [L, 256, 15, NS])
    st_qkv_d = din("st_qkv", [L, 1536, 3, NS])
    st_S_d = din("st_S", [L, NS, 4, 128, 128])
    st_conv_d = din("st_conv", [L, 256, 30, NS])

    yT_d = dout("yT", [D, T])
    ysT_d = dout("ysT", [D, NS])
    o_pool_p = dout("o_pool_p", [L, 256, 15])
    o_pool_s = dout("o_pool_s", [L, 256, 15, NS])
    o_qkv_p = dout("o_qkv_p", [L, 1536, 3])
    o_qkv_s = dout("o_qkv_s", [L, 1536, 3, NS])
    o_S_p = dout("o_S_p", [L, 4, 128, 128])
    o_S_s = dout("o_S_s", [L, NS, 4, 128, 128])
    o_conv_p = dout("o_conv_p", [L, 256, 30])
    o_conv_s = dout("o_conv_s", [L, 256, 30, NS])

    def sb(name, shape, dt=F32):
        return es.enter_context(nc.sbuf_tensor("sb_" + name, list(shape), dt))

    def op(eng, meth, r, w, *args, **kw):
        S.add(eng, meth, r, w, args, kw)

    def dma(q, r, w, out, in_):
        S.add(q, "dma_start", r, w, (), dict(out=out, in_=in_), dma=True)

    def V(meth, r, w, **kw):
        op("dve", meth, r, w, **kw)

    def A(r, w, out, in_, func, **kw):
        op("act", "activation", r, w, out=out, in_=in_, func=func, **kw)

    def G(meth, r, w, **kw):
        op("pool", meth, r, w, **kw)

    def MM(r, w, out, lhsT, rhs, start=True, stop=True):
        op("pe", "matmul", r, w, out, lhsT=lhsT, rhs=rhs, start=start, stop=stop)

    def TR(r, w, out, in_, ident):
        op("pe", "transpose", r, w, out, in_, ident)

    psb = [es.enter_context(nc.psum_tensor("ps%d" % i, [128, 512], F32)) for i in range(8)]

    class PS:
        def __init__(self, bank, c0, n):
            self.bank, self.c0, self.n = bank, c0, n
            self.k = [("psbank", bank)]

        def ap(self, p=128, n=None, off=0):
            n = self.n if n is None else n
            return psb[self.bank][0:p, self.c0 + off:self.c0 + off + n]

    rot = {"H": 0, "Q": 0, "F": 0}

    def psH():
        i = rot["H"]
        rot["H"] = (i + 1) % 2
        return PS(i, 0, 256)

    def psQ():
        i = rot["Q"]
        rot["Q"] = (i + 1) % 5
        return PS(2 + i, 0, 128)

    def psF():
        return PS(7, 0, 512)

    rotF8 = [0]

    def psF8():
        i = rotF8[0]
        rotF8[0] = (i + 1) % 8
        return PS(i, 0, 512)

    xT = sb("xT", [128, KC, T])
    xs = sb("xs", [128, KC, NS])
    cst = sb("cst", [128, NCST])
    ones16 = sb("ones16", [128, 128], BF16)
    pp = sb("pp", [128, NPAR])
    gf = sb("gf", [128, KC])
    mod = sb("mod", [128, 48, 17])
    lay = sb("lay", [128, 64])
    A1s = sb("A1s", [128, KC, NS])
    A2s = sb("A2s", [128, KC, NS])
    rstd = sb("rstd", [128, 512])
    tmpA = sb("tmpA", [128, 512])
    tmpB = [sb("tmpB%d" % i, [128, 512]) for i in range(2)]
    comb_ap = tmpB[1][0:8, 0:TT]
    sqv = [tmpB[i][:, :].bitcast(BF16).rearrange("p (k n) -> p k n", k=4) for i in range(2)]
    sS = sb("sS", [128, 15, 64])
    small = sb("small", [128, 16])
    ARENA_W = 31300
    arena = sb("arena", [128, ARENA_W])

    class Arena:
        def __init__(self):
            self.off = 0
            self.hi = 0

        def frame(self, off=0):
            self.off = off

        def alloc(self, shape, dt=F32):
            n = int(np.prod(shape[1:]))
            words = n if dt == F32 else (n + 1) // 2
            a = self.off
            self.off += words
            self.hi = max(self.hi, self.off)
            assert self.off <= ARENA_W, ("arena overflow", self.off)
            v = arena[:, a:a + words]
            if dt != F32:
                v = v.bitcast(dt)[:, 0:n]
            if len(shape) == 3:
                v = v.rearrange("p (a b) -> p a b", a=shape[1])
            elif len(shape) == 4:
                v = v.rearrange("p (a b c) -> p a b c", a=shape[1], b=shape[2])
            if shape[0] < 128:
                v = v[0:shape[0]]
            return v

    AR = Arena()
    w_in16 = AR.alloc([128, KC, NIN], BF16)
    w_out16 = AR.alloc([128, KC, D], BF16)
    pwbd16 = AR.alloc([128, 2, 128], BF16)
    cpw16 = AR.alloc([128, 2, 256], BF16)
    W_END = AR.off
    hT = AR.alloc([128, KC, TT], BF16)
    mixT = AR.alloc([128, KC, TT], BF16)
    pext = AR.alloc([128, 2, 15 + TT])
    ptA = [AR.alloc([128, 15 + TT])] * 2
    ptB = [AR.alloc([128, 15 + TT])] * 2
    pd16_ = AR.alloc([128, TT], BF16)
    pd16 = pd16_.unsqueeze(1).to_broadcast([128, 2, TT])
    gext = AR.alloc([128, 2, 30 + TT])
    cacc = [AR.alloc([128, TT])] * 2
    ccen = [AR.alloc([128, TT])] * 2
    cac2 = [AR.alloc([128, TT])] * 2
    csc16 = AR.alloc([128, 2, TT], BF16)
    qhist = AR.alloc([128, 12, 3])
    qext = [AR.alloc([128, 3, 3 + TT])] * 2
    qkvb = [AR.alloc([128, 3, TT]) for i in range(2)]
    zgs = [AR.alloc([128, TT]) for i in range(2)]
    sqh = [AR.alloc([128, 2, TT], BF16) for i in range(2)]
    brow = tmpA[0:8, 0:TT]
    grow = tmpA[0:8, TT:2 * TT]
    comb = None
    bg = AR.alloc([128, 2, 8])
    tk = AR.alloc([128, 2, 6, 4])
    Sst = AR.alloc([128, 4, 2, 128])
    NSET = 4
    CMN = ("ktm", "vtm", "mg", "dec", "dinc", "qkd", "pa", "pta", "pb", "ptb", "za", "zb")
    cm = [dict((n, AR.alloc([128, 128])) for n in CMN) for i in range(NSET)]
    AR.frame(W_END)
    hs = AR.alloc([128, KC, NS], BF16)
    mixS = AR.alloc([128, KC, NS], BF16)
    sq16s = AR.alloc([128, KC, NS], BF16)
    pd16s = AR.alloc([128, NS], BF16)
    csc16s = AR.alloc([128, 2, NS], BF16)
    projS = AR.alloc([128, 23, NS])
    pexS = AR.alloc([128, 2, 16, NS])
    cexS = AR.alloc([128, 2, 31, NS])
    qexS = AR.alloc([128, 12, 4, NS])
    sprod = AR.alloc([128, 12, 4, NS])
    cprod = AR.alloc([128, 31, NS])
    qkvS = AR.alloc([128, 12, NS])
    browS = AR.alloc([8, NS])
    growS = AR.alloc([8, NS])
    combS = AR.alloc([8, NS])
    selS = AR.alloc([8, 2, NS])
    Sin = AR.alloc([128, NS, 128])
    Sout = AR.alloc([128, NS, 128])
    kexp = AR.alloc([16, 2, 128])
    ktmS = AR.alloc([16, 128])
    utmS = AR.alloc([16, 128])
    AR.frame(0)
    HP = 512
    NHP = DFF // HP
    hf = AR.alloc([128, KC, T + NS], BF16)
    apart = [AR.alloc([128, HP // 128, T + NS], BF16) for i in range(2)]
    w1p = [AR.alloc([128, KC, HP], BF16) for i in range(2)]
    w2p = [AR.alloc([128, HP // 128, D], BF16) for i in range(2)]
    sqf = AR.alloc([128, KC, 512], BF16)
    AR.frame(0)
    c32 = AR.alloc([128, KC, 17])
    csg = AR.alloc([128, KC, 17])
    sc16 = AR.alloc([128, KC, 17], BF16)
    wad = [AR.alloc([128, KC, 1024], BF16) for i in range(2)]

    ident = cst[:, C_ID:C_ID + 128]
    Uinc = cst[:, C_U:C_U + 128]
    Mstrict = cst[:, C_MS:C_MS + 128]
    Tstrict = cst[:, C_TS:C_TS + 128]
    ones32 = cst[:, C_ONES:C_ONES + 128]
    blk64 = cst[:, C_BLK:C_BLK + 128]
    bd16 = cst[:, C_BD:C_BD + 128]
    epsc = cst[:, C_EPS:C_EPS + 1]
    onec = cst[:, C_ONE:C_ONE + 1]

    def xk(kc, c0, n):
        return [("x", kc, t) for t in range(c0 // TT, (c0 + n - 1) // TT + 1)]

    def barrier():
        S.barrier(lay[:, 63:64])

    dma("sp", [], ["cst"], cst[:], cst_d)
    dma("sp", [], ["gf"], gf[:], gf_d)
    for kc in range(KC):
        dma("sp", [], xk(kc, 0, T), xT[:, kc, :], xT_d[kc * 128:(kc + 1) * 128, :])
    dma("sp", [], ["xs"], xs[:], xsT_d.rearrange("(k p) n -> p k n", p=128))
    V("tensor_copy", ["cst"], ["ones16"], out=ones16[:], in_=ones32)

    def ada_phase(l):
        dma("sp", [], ["pp"], pp[:], pp_d[:, l, :])
        dma("sp", [], ["c32"], c32, cT_d.rearrange("(k p) n -> p k n", p=128))
        sigm(["c32"], "csg", csg, c32)
        V("tensor_tensor", ["c32", "csg"], ["sc16"], out=sc16, in0=c32, in1=csg, op=ALU.mult)
        for g in range(6):
            wb = wad[g % 2]
            wk = "wad%d" % (g % 2)
            dma("pool", [], [wk], wb, w_ada_d[l, :, g * 1024:(g + 1) * 1024].rearrange("(k p) n -> p k n", p=128))
            ps = psF()
            for jc in range(8):
                for kc in range(KC):
                    MM([wk, "sc16"], ps.k, ps.ap(128, 17, jc * 17), wb[:, kc, jc * 128:(jc + 1) * 128],
                       sc16[:, kc, :], start=(kc == 0), stop=(kc == KC - 1))
            for jc in range(8):
                j = g * 8 + jc
                if g in (1, 4):
                    V("tensor_scalar", ps.k + ["pp"], ["mod"], out=mod[:, j, :], in0=ps.ap(128, 17, jc * 17),
                      scalar1=pp[:, P_BADA + j:P_BADA + j + 1], scalar2=1.0, op0=ALU.add, op1=ALU.add)
                else:
                    V("tensor_scalar", ps.k + ["pp"], ["mod"], out=mod[:, j, :], in0=ps.ap(128, 17, jc * 17),
                      scalar1=pp[:, P_BADA + j:P_BADA + j + 1], scalar2=None, op0=ALU.add)

    def layer_params(l):
        V("tensor_tensor", ["pp", "mod"], ["lay"], out=lay[:, 0:8], in0=pp[:, P_G1:P_G1 + 8], in1=mod[:, 8:16, 0], op=ALU.mult)
        V("tensor_tensor", ["pp", "mod"], ["lay"], out=lay[:, 8:16], in0=pp[:, P_G2:P_G2 + 8], in1=mod[:, 32:40, 0], op=ALU.mult)
        V("tensor_tensor", ["pp", "mod"], ["A1s"], out=A1s[:], in0=mod[:, 8:16, 1:17],
          in1=pp[:, P_G1:P_G1 + 8].unsqueeze(2).to_broadcast([128, 8, NS]), op=ALU.mult)
        V("tensor_tensor", ["pp", "mod"], ["A2s"], out=A2s[:], in0=mod[:, 32:40, 1:17],
          in1=pp[:, P_G2:P_G2 + 8].unsqueeze(2).to_broadcast([128, 8, NS]), op=ALU.mult)
        A(["pp"], ["lay16"], lay[0:8, 16:17], pp[0:8, P_ALOG:P_ALOG + 1], AF.Exp)
        V("tensor_scalar", ["lay16"], ["lay16"], out=lay[0:8, 16:17], in0=lay[0:8, 16:17], scalar1=-1.0, scalar2=None, op0=ALU.mult)

    WGRP = ((0, 256, "pool"), (256, 768, "q"), (768, 1280, "k"), (1280, 1792, "v"), (1792, 2312, "z"), (2312, 2824, "glu"))

    def load_layer_weights(l):
        for (a, b, key) in WGRP:
            dma("pool", [], [("win", key)], w_in16[:, :, a:b], w_in_d[l, :, a:b].rearrange("(k p) n -> p k n", p=128))
        dma("pool", [], ["wout"], w_out16, w_out_d[l].rearrange("(k p) n -> p k n", p=128))
        dma("pool", [], ["cpw"], cpw16, pw_d[l].rearrange("(k p) n -> p k n", p=128))
        op("pool", "memset", [], ["pwbd16"], pwbd16, 0.0)
        for g in range(4):
            c, hf_ = g // 2, g % 2
            dma("pool", ["pwbd16"], [("pwbd", g)], pwbd16[hf_ * 64:(hf_ + 1) * 64, c, hf_ * 64:(hf_ + 1) * 64], pool_w_d[l, g])

    PWK = ["pwbd16"] + [("pwbd", g) for g in range(4)]

    def win_key(col):
        for (a, b, key) in WGRP:
            if a <= col < b:
                return ("win", key)

    def proj(ps, col, m, rhs_t, rkey, n):
        for kc in range(KC):
            MM([win_key(col), (rkey, kc)], ps.k, ps.ap(m, n), w_in16[:, kc, col:col + m], rhs_t[:, kc, 0:n],
               start=(kc == 0), stop=(kc == KC - 1))

    def rms_rstd(ps, n, scale, w_key, out_ap):
        A(ps.k + ["cst"], ["tmpA"], tmpA[:, 0:n], ps.ap(128, n), AF.Ln, bias=epsc, scale=scale)
        A(["tmpA"], [w_key], out_ap, tmpA[:, 0:n], AF.Exp, scale=-0.5)

    def sigm(r, key, buf, src, p=128):
        A(r, [key], buf, src, AF.Exp, scale=-1.0)
        A([key, "cst"], [key], buf, buf, AF.Ln, bias=onec[0:p, :], scale=1.0)
        A([key], [key], buf, buf, AF.Exp, scale=-1.0)

    def run_ls(gens):
        base = min(S.efree.values()) if S.efree else 0.0
        ready = [base] * len(gens)
        alive = [True] * len(gens)
        blocked = [False] * len(gens)
        spins = 0
        while any(alive):
            i = min((r, j) for j, r in enumerate(ready) if alive[j])[1]
            n0 = len(S.ops)
            S.stage_fin = 0.0
            try:
                next(gens[i])
            except StopIteration:
                alive[i] = False
                continue
            if len(S.ops) == n0:
                blocked[i] = True
                others = [ready[j] for j in range(len(gens)) if alive[j] and j != i and not blocked[j]]
                if not others:
                    for j in range(len(gens)):
                        blocked[j] = False
                    spins += 1
                    assert spins < 1000, "list scheduler deadlock"
                    ready[i] += 1.0
                else:
                    ready[i] = min(others) + 50.0
            else:
                blocked[i] = False
                ready[i] = S.stage_fin
                spins = 0

    def run_rr(gens, weights=None):
        act = [(g, (weights[i] if weights else 1)) for i, g in enumerate(gens)]
        while act:
            for item in list(act):
                g, w = item
                try:
                    for _ in range(w):
                        next(g)
                except StopIteration:
                    act.remove(item)

    def rr_gen(gens):
        act = list(gens)
        while act:
            for g in list(act):
                try:
                    next(g)
                    yield
                except StopIteration:
                    act.remove(g)

    def rms2(ps, n, scale, tmp_ap, tmp_key, out_ap, out_key):
        A(ps.k + ["cst"], [tmp_key], tmp_ap, ps.ap(128, n), AF.Ln, bias=epsc, scale=scale)
        A([tmp_key], [out_key], out_ap, tmp_ap, AF.Exp, scale=-0.5)

    def pool_unit(l, it, c):
        E = 15 + TT
        ek = ("pext", c)
        k1, k2 = "pt1", "pt2"
        p1_, p2_ = ptA[c], ptB[c]
        if it == 0:
            V("memset", [], [ek], ap=pext[:, c, 0:15], constant=0.0)
        else:
            V("tensor_copy", [ek], [ek], out=pext[:, c, 0:15], in_=pext[:, c, TT:TT + 15])
        ps = psH()
        proj(ps, c * 128, 128, hT, "hT", TT)
        A(ps.k, [ek], pext[:, c, 15:E], ps.ap(), AF.Copy)
        yield
        e = pext[:, c, :]
        G("tensor_tensor", [ek], [k1], out=p1_[:, 1:E], in0=e[:, 1:E], in1=e[:, 0:E - 1], op=ALU.add)
        yield
        if c == 0:
            G("tensor_tensor", [k1], [k2], out=p2_[64:128, 3:E], in0=p1_[64:128, 3:E], in1=p1_[64:128, 1:E - 2], op=ALU.add)
        else:
            G("tensor_tensor", [k1], [k2], out=p2_[:, 3:E], in0=p1_[:, 3:E], in1=p1_[:, 1:E - 2], op=ALU.add)
            yield
            G("tensor_tensor", [k2], [k1], out=p1_[:, 7:E], in0=p2_[:, 7:E], in1=p2_[:, 3:E - 4], op=ALU.add)
            yield
            G("tensor_tensor", [k1], [k2], out=p2_[64:128, 15:E], in0=p1_[64:128, 15:E], in1=p1_[64:128, 7:E - 8], op=ALU.add)
        yield
        V("tensor_scalar", [k1, "cst"], [k1], out=p1_[0:64, 15:E], in0=p1_[0:64, 15:E],
          scalar1=cst[0:64, C_INVW + c:C_INVW + c + 1], scalar2=None, op0=ALU.mult)
        V("tensor_scalar", [k2, "cst", k1], [k1], out=p1_[64:128, 15:E], in0=p2_[64:128, 15:E],
          scalar1=cst[64:128, C_INVW + c:C_INVW + c + 1], scalar2=None, op0=ALU.mult)
        yield
        if it == 0:
            G("tensor_tensor", [k1, "cst"], [k1], out=p1_[:, 15:30], in0=p1_[:, 15:30],
              in1=cst[:, C_FIX + c * 15:C_FIX + c * 15 + 15], op=ALU.mult)
        G("tensor_tensor", [k1, ek], ["pd16"], out=pd16[:, c, :], in0=p1_[:, 15:E], in1=pext[:, c, 15:E], op=ALU.subtract)
        yield
        ps2 = psH()
        MM(["pd16"] + PWK, ps2.k, ps2.ap(), pwbd16[:, c, :], pd16[:, c, :])
        A(ps2.k + ["pp"], [("mixT", c)], mixT[:, c, :], ps2.ap(), AF.Copy, scale=pp[:, P_PSC + c:P_PSC + c + 1])
        if it == NTILE - 1:
            dma("sp", [ek], [], o_pool_p[l, c * 128:(c + 1) * 128, :], pext[:, c, TT:TT + 15])
        yield

    def conf_unit(l, it, c):
        ek = ("gext", c)
        ka, kc_, kp = "cacc", "ccen", "cac2"
        acc, cen, ac2 = cacc[c], ccen[c], cac2[c]
        if it == 0:
            V("memset", [], [ek], ap=gext[:, c, 0:30], constant=0.0)
        else:
            V("tensor_copy", [ek], [ek], out=gext[:, c, 0:30], in_=gext[:, c, TT:TT + 30])
        psa = psH()
        proj(psa, 2312 + c * 128, 128, hT, "hT", TT)
        psg = psH()
        proj(psg, 2568 + c * 128, 128, hT, "hT", TT)
        sigm(psg.k, kc_, cen, psg.ap())
        V("tensor_tensor", psa.k + [kc_], [ek], out=gext[:, c, 30:30 + TT], in0=psa.ap(), in1=cen, op=ALU.mult)
        yield
        wc = P_CW + c * 31
        accs = [(acc, ka), (ac2, kp), (cen, kc_)]
        V("tensor_scalar", [ek, "pp"], [ka], out=acc, in0=gext[:, c, 0:TT], scalar1=pp[:, wc:wc + 1],
          scalar2=pp[:, P_CB + c:P_CB + c + 1], op0=ALU.mult, op1=ALU.add)
        V("tensor_scalar", [ek, "pp"], [kp], out=ac2, in0=gext[:, c, 1:1 + TT], scalar1=pp[:, wc + 1:wc + 2], scalar2=None, op0=ALU.mult)
        V("tensor_scalar", [ek, "pp"], [kc_], out=cen, in0=gext[:, c, 2:2 + TT], scalar1=pp[:, wc + 2:wc + 3], scalar2=None, op0=ALU.mult)
        yield
        for k in range(3, 31):
            ab, akey = accs[k % 3]
            V("scalar_tensor_tensor", [ek, "pp", akey], [akey], out=ab, in0=gext[:, c, k:k + TT], scalar=pp[:, wc + k:wc + k + 1],
              in1=ab, op0=ALU.mult, op1=ALU.add)
            if k % 3 == 2:
                yield
        V("tensor_tensor", [ka, kp], [ka], out=acc, in0=acc, in1=ac2, op=ALU.add)
        V("tensor_tensor", [ka, kc_], [ka], out=acc, in0=acc, in1=cen, op=ALU.add)
        yield
        psm = psH()
        MM([ka, "cst"], psm.k, psm.ap(), blk64, acc)
        V("tensor_tensor", [ka] + psm.k, [kc_], out=cen, in0=acc, in1=psm.ap(), op=ALU.subtract)
        yield
        A([kc_], [ka], acc, cen, AF.Square)
        yield
        psv = psH()
        MM([ka, "cst"], psv.k, psv.ap(), blk64, acc)
        rms2(psv, TT, 1.0, acc, ka, acc, ka)
        V("tensor_tensor", [kc_, ka], [kc_], out=cen, in0=cen, in1=acc, op=ALU.mult)
        yield
        A([kc_, "pp"], [ka], acc, cen, AF.Identity, scale=pp[:, P_LG + c:P_LG + c + 1], bias=pp[:, P_LB + c:P_LB + c + 1])
        sigm([ka], kc_, cen, acc)
        V("tensor_tensor", [ka, kc_], [("csc16", c)], out=csc16[:, c, :], in0=acc, in1=cen, op=ALU.mult)
        if it == NTILE - 1:
            dma("sp", [ek], [], o_conv_p[l, c * 128:(c + 1) * 128, :], gext[:, c, TT:TT + 30])
        yield

    def conf_final(l, it):
        for oc in range(2):
            ps = psH()
            for c in range(2):
                MM([("csc16", c), "cpw"], ps.k, ps.ap(), cpw16[:, c, oc * 128:(oc + 1) * 128], csc16[:, c, :],
                   start=(c == 0), stop=(c == 1))
            A(ps.k, [("mixT", 6 + oc)], mixT[:, 6 + oc, :], ps.ap(), AF.Copy)
            yield

    def side_thread(l, it):
        def pools():
            yield from pool_unit(l, it, 0)
            yield from pool_unit(l, it, 1)
        def confs():
            yield from conf_unit(l, it, 0)
            yield from conf_unit(l, it, 1)
        yield from rr_gen([pools(), confs()])
        yield from conf_final(l, it)
        hstate["side_done"] = True
        yield

    def ba_unit(l, it):
        psb_ = psH()
        proj(psb_, 2304, 8, hT, "hT", TT)
        sigm(psb_.k, "tmpA", brow, psb_.ap(8), p=8)
        A(psb_.k + ["pp"], ["tmpA"], grow, psb_.ap(8), AF.Exp, bias=pp[0:8, P_DTB:P_DTB + 1], scale=1.0)
        A(["tmpA", "cst"], ["tmpA"], grow, grow, AF.Ln, bias=onec[0:8, :], scale=1.0)
        V("tensor_scalar", ["tmpA", "lay16", "cst"], ["tmpA"], out=grow, in0=grow, scalar1=lay[0:8, 16:17],
          scalar2=cst[0:8, C_M47:C_M47 + 1], op0=ALU.mult, op1=ALU.mult)
        V("scalar_tensor_tensor", ["tmpA", "tmpA", "cst"], [("tmpB", 1)], out=comb_ap, in0=brow,
          scalar=cst[0:8, C_M03:C_M03 + 1], in1=grow, op0=ALU.mult, op1=ALU.add)
        for ci in range(2):
            pq = psQ()
            TR([("tmpB", 1), "cst"], pq.k, pq.ap(128, 8), comb_ap[:, ci * 128:(ci + 1) * 128], ident[0:8, 0:8])
            V("tensor_copy", pq.k, [("bg", ci)], out=bg[:, ci, :], in_=pq.ap(128, 8))
            pq2 = psQ()
            MM([("bg", ci), "cst"], pq2.k, pq2.ap(128, 8), Uinc, bg[:, ci, :])
            MM([("bg", ci), "cst"], pq2.k, pq2.ap(128, 8, 8), ones32, bg[:, ci, :])
            tkk = ("tk", ci)
            V("tensor_copy", pq2.k, [tkk], out=tk[:, ci, 5, :], in_=pq2.ap(128, 4, 4))
            A(pq2.k, [tkk], tk[:, ci, 0, :], pq2.ap(128, 4, 4), AF.Exp)
            A(pq2.k, [tkk], tk[:, ci, 3, :], pq2.ap(128, 4, 12), AF.Exp)
            V("tensor_scalar", [tkk], [tkk], out=tk[:, ci, 1, :], in0=tk[:, ci, 0, :], scalar1=-1.0, scalar2=None, op0=ALU.mult)
            V("tensor_tensor", pq2.k + [tkk], [("small", 0)], out=small[:, 0:4], in0=pq2.ap(128, 4, 12), in1=tk[:, ci, 5, :],
              op=ALU.subtract)
            A([("small", 0)], [tkk], tk[:, ci, 2, :], small[:, 0:4], AF.Exp)
            V("tensor_scalar", [("bg", ci)], [tkk], out=tk[:, ci, 4, :], in0=bg[:, ci, 0:4], scalar1=-1.0, scalar2=None,
              op0=ALU.mult)

    hstate = {}

    def ba_gen(l, it):
        ba_unit(l, it)
        hstate["ba_done"] = True
        yield

    def head_front(l, it, h):
        b = h % 2
        qx, qv, zgb = qext[b], qkvb[b], zgs[b]
        kx, kv, kz = "qext", ("qkv", b), ("zg", b)
        while hstate.get(("chunks_done", it, h - 2), h < 2) is not True and h >= 2:
            yield
        if it == 0:
            V("memset", [], [("qhist", h)], ap=qhist[:, h * 3:(h + 1) * 3, :], constant=0.0)
        V("tensor_copy", [("qhist", h)], [kx], out=qx[:, :, 0:3], in_=qhist[:, h * 3:(h + 1) * 3, :])
        for j in range(3):
            ps = psH()
            proj(ps, 256 + j * 512 + h * 128, 128, hT, "hT", TT)
            A(ps.k, [kx], qx[:, j, 3:3 + TT], ps.ap(), AF.Copy)
            yield
        V("tensor_copy", [kx], [("qhist", h)], out=qhist[:, h * 3:(h + 1) * 3, :], in_=qx[:, :, TT:TT + 3])
        if it == NTILE - 1:
            for j in range(3):
                dma("sp", [("qhist", h)], [], o_qkv_p[l, j * 512 + h * 128:j * 512 + (h + 1) * 128, :], qhist[:, h * 3 + j, :])
        psz = psH()
        proj(psz, 1792 + h * 128, 128, hT, "hT", TT)
        sigm(psz.k, kz, zgb, psz.ap())
        V("scalar_tensor_tensor", psz.k + [kz, "pp"], [kz], out=zgb, in0=psz.ap(), scalar=pp[:, P_DG:P_DG + 1], in1=zgb,
          op0=ALU.mult, op1=ALU.mult)
        yield
        for k in range(4):
            for j in range(3):
                wq = P_QW + (j * 4 + h) * 4
                kvj = ("qkvj", b, j)
                if k == 0:
                    V("tensor_scalar", [kx, "pp"], [kvj, kv], out=qv[:, j, :], in0=qx[:, j, 0:TT], scalar1=pp[:, wq:wq + 1], scalar2=None,
                      op0=ALU.mult)
                else:
                    V("scalar_tensor_tensor", [kx, "pp", kvj], [kvj] + ([kv] if k == 3 else []), out=qv[:, j, :], in0=qx[:, j, k:k + TT],
                      scalar=pp[:, wq + k:wq + k + 1], in1=qv[:, j, :], op0=ALU.mult, op1=ALU.add)
            yield
        sigm([kv], kx, qx[:, :, 0:TT], qv)
        V("tensor_tensor", [kv, kx], [kv], out=qv, in0=qv, in1=qx[:, :, 0:TT], op=ALU.mult)
        A([kv], [("sqh", b)], sqh[b], qv[:, 0:2, :], AF.Square)
        yield
        psn = psF()
        MM([("sqh", b), "ones16"], psn.k, psn.ap(), ones16[:], sqh[b])
        rms2(psn, 2 * TT, 1.0, tmpA[:, 0:2 * TT], "tmpA", rstd[:, 0:2 * TT], "rstd")
        yield
        V("scalar_tensor_tensor", [kv, "rstd"], [kv], out=qv[:, 0, :], in0=qv[:, 0, :], scalar=128.0 ** -0.5,
          in1=rstd[:, 0:TT], op0=ALU.mult, op1=ALU.mult)
        V("tensor_tensor", [kv, "rstd"], [kv], out=qv[:, 1, :], in0=qv[:, 1, :], in1=rstd[:, TT:2 * TT], op=ALU.mult)
        hstate[("front_done", it, h)] = True
        yield

    def prep_thread(l, it, par):
        for u in range(par, 8, 4):
            h, ci = u // 2, u % 2
            while (hstate.get("ba_done") is not True or hstate.get(("front_done", it, h)) is not True
                   or (u >= 4 and hstate.get(("seq_done", u - 4)) is not True)):
                yield
            yield from chunk_prep(l, it, h, ci)
            hstate[("prep_done", u)] = True
            yield

    def seq_thread(l, it, hp):
        for u in (2 * hp, 2 * hp + 1, 2 * hp + 4, 2 * hp + 5):
            h, ci = u // 2, u % 2
            while hstate.get(("prep_done", u)) is not True:
                yield
            yield from chunk_seq(l, it, h, ci)
            hstate[("seq_done", u)] = True
            if ci == 1:
                if it == NTILE - 1:
                    fin = (NTILE * 2) % 2
                    dma("sp", [("S", h, fin)], [], o_S_p[l, h], Sst[:, h, fin, :])
                hstate[("chunks_done", it, h)] = True
            yield

    def fronts_thread(l, it):
        for h in range(4):
            yield from head_front(l, it, h)


    ln_done = {}

    def ln_ops(l, it):
        c0 = it * TT
        xkeys = [k for kc in range(KC) for k in xk(kc, c0, TT)]
        for i2 in range(2):
            A(xkeys, [("tmpB", i2)], sqv[i2], xT[:, 4 * i2:4 * i2 + 4, c0:c0 + TT], AF.Square)
        yield
        ps = psH()
        for kc in range(KC):
            MM([("tmpB", kc // 4), "ones16"], ps.k, ps.ap(), ones16[:], sqv[kc // 4][:, kc % 4, :], start=(kc == 0), stop=(kc == KC - 1))
        rms_rstd(ps, TT, 1.0 / D, "rstd", rstd[:, 0:TT])
        yield
        for kc in range(KC):
            V("scalar_tensor_tensor", xk(kc, c0, TT) + ["rstd", "lay"], [("tmpB", kc % 2)],
              out=tmpB[kc % 2][:, 0:TT], in0=xT[:, kc, c0:c0 + TT], scalar=lay[:, kc:kc + 1], in1=rstd[:, 0:TT],
              op0=ALU.mult, op1=ALU.mult)
            A([("tmpB", kc % 2), "mod"], [("hT", kc)], hT[:, kc, :], tmpB[kc % 2][:, 0:TT], AF.Identity,
              bias=mod[:, kc, 0:1], scale=1.0)
            yield
        ln_done[(l, it)] = True

    def ln_next(l, it):
        while not (hstate.get("ba_done") is True and hstate.get("side_done") is True
                   and all(hstate.get(("front_done", it, h)) is True for h in range(4))):
            yield
        yield from ln_ops(l, it + 1)

    def mix_prompt_tile(l, it):
        if (l, it) not in ln_done:
            for _ in ln_ops(l, it):
                pass
        if it > 0:
            out_proj_tile(l, it - 1)
        hstate.clear()
        if USE_LS:
            run_ls([ba_gen(l, it), fronts_thread(l, it), prep_thread(l, it, 0), prep_thread(l, it, 1), prep_thread(l, it, 2),
                    prep_thread(l, it, 3), seq_thread(l, it, 0), seq_thread(l, it, 1), side_thread(l, it)]
                   + ([ln_next(l, it)] if it + 1 < NTILE else []))
        else:
            run_rr([ba_gen(l, it), fronts_thread(l, it), prep_thread(l, it, 0), prep_thread(l, it, 1), prep_thread(l, it, 2),
                    prep_thread(l, it, 3), seq_thread(l, it, 0), seq_thread(l, it, 1), side_thread(l, it)],
                   weights=[1, 2, 2, 2, 2, 2, 2, 2, 3])

    def out_proj_tile(l, it):
        c0 = it * TT
        for oc in range(KC):
            ps = psH()
            for kc in range(KC):
                MM(["wout", ("mixT", kc)], ps.k, ps.ap(), w_out16[:, kc, oc * 128:(oc + 1) * 128], mixT[:, kc, :],
                   start=(kc == 0), stop=(kc == KC - 1))
            V("scalar_tensor_tensor", ps.k + xk(oc, c0, TT) + ["mod"], xk(oc, c0, TT), out=xT[:, oc, c0:c0 + TT],
              in0=ps.ap(), scalar=mod[:, 16 + oc, 0:1], in1=xT[:, oc, c0:c0 + TT], op0=ALU.mult, op1=ALU.add)

    def chunk_ctx(it, h, ci):
        b = h % 2
        si = (2 * h + ci) % NSET
        m = cm[si]
        K = lambda n: "cm%d_%s" % (si, n)
        cc = ci * 128
        qv = qkvb[b]
        return m, K, cc, qv[:, 0, cc:cc + 128], qv[:, 1, cc:cc + 128], qv[:, 2, cc:cc + 128], ("qkv", b), ("tk", ci)

    def chunk_prep(l, it, h, ci):
        m, K, cc, qTc, kTc, vTc, kv, tkk = chunk_ctx(it, h, ci)
        sc = lambda i: tk[:, ci, i, h:h + 1]
        p1 = psQ()
        TR([kv, "cst"], p1.k, p1.ap(), kTc, ident)
        A(p1.k + [tkk], [K("ktm")], m["ktm"], p1.ap(), AF.Copy, scale=sc(2))
        p2 = psQ()
        TR([kv, "cst"], p2.k, p2.ap(), vTc, ident)
        V("tensor_copy", p2.k, [K("vtm")], out=m["vtm"], in_=p2.ap())
        yield
        V("tensor_scalar", ["cst", ("bg", ci)], [K("mg")], out=m["mg"], in0=Mstrict, scalar1=bg[:, ci, 4 + h:5 + h],
          scalar2=None, op0=ALU.mult)
        p3 = psQ()
        MM([K("mg"), "cst"], p3.k, p3.ap(), m["mg"], Uinc)
        A(p3.k, [K("dec")], m["dec"], p3.ap(), AF.Exp)
        yield
        G("tensor_tensor", [K("dec"), "cst"], [K("dinc")], out=m["dinc"], in0=m["dec"], in1=Uinc, op=ALU.mult)
        p5 = psQ()
        MM([kv], p5.k, p5.ap(), kTc, qTc)
        V("tensor_tensor", p5.k + [K("dinc")], [K("qkd")], out=m["qkd"], in0=p5.ap(), in1=m["dinc"], op=ALU.mult)
        yield
        G("tensor_tensor", [K("dec"), "cst"], [K("dinc")], out=m["dinc"], in0=m["dec"], in1=Tstrict, op=ALU.mult)
        p4 = psQ()
        MM([kv], p4.k, p4.ap(), kTc, kTc)
        V("scalar_tensor_tensor", p4.k + [K("dinc"), tkk], [K("ptb")], out=m["ptb"], in0=p4.ap(), scalar=sc(4),
          in1=m["dinc"], op0=ALU.mult, op1=ALU.mult)
        yield
        p6 = psQ()
        TR([K("ptb"), "cst"], p6.k, p6.ap(), m["ptb"], ident)
        A(p6.k, [K("pb")], m["pb"], p6.ap(), AF.Copy)
        yield
        G("tensor_tensor", [K("ptb"), "cst"], [K("pa")], out=m["pa"], in0=m["ptb"], in1=bd16, op=ALU.mult)
        G("tensor_tensor", [K("pb"), "cst"], [K("pta")], out=m["pta"], in0=m["pb"], in1=bd16, op=ALU.mult)
        G("tensor_tensor", [K("ptb"), K("pa")], [K("mg")], out=m["mg"], in0=m["ptb"], in1=m["pa"], op=ALU.subtract)
        V("tensor_tensor", [K("pa"), "cst"], [K("za")], out=m["za"], in0=m["pa"], in1=ident, op=ALU.add)
        yield

        def mm_ev(dst, lhs, rhs, eng):
            pq_ = psQ()
            MM([K(lhs), K(rhs)], pq_.k, pq_.ap(), m[lhs], m[rhs])
            if eng == "act":
                A(pq_.k, [K(dst)], m[dst], pq_.ap(), AF.Copy)
            else:
                V("tensor_copy", pq_.k, [K(dst)], out=m[dst], in_=pq_.ap())

        def mm_acc(dst, lhs, rhs):
            pq_ = psQ()
            MM([K(lhs), K(rhs)], pq_.k, pq_.ap(), m[lhs], m[rhs])
            V("tensor_tensor", pq_.k + [K(rhs)], [K(dst)], out=m[dst], in0=pq_.ap(), in1=m[rhs], op=ALU.add)

        mm_ev("ptb", "pa", "pta", "act")
        mm_ev("pb", "pta", "pa", "dve")
        yield
        mm_acc("zb", "ptb", "za")
        mm_ev("pta", "pb", "ptb", "act")
        yield
        mm_ev("pa", "ptb", "pb", "dve")
        mm_acc("za", "pta", "zb")
        yield
        mm_ev("ptb", "pa", "pta", "act")
        yield
        mm_acc("zb", "ptb", "za")
        yield
        pq_ = psQ()
        TR([K("zb"), "cst"], pq_.k, pq_.ap(), m["zb"], ident)
        A(pq_.k, [K("dec")], m["dec"], pq_.ap(), AF.Copy)
        yield
        mm_ev("pa", "dec", "mg", "dve")
        mm_ev("pta", "mg", "dec", "act")
        yield
        mm_acc("za", "pta", "zb")
        mm_ev("ptb", "pa", "pta", "act")
        yield
        mm_ev("pb", "pta", "pa", "dve")
        mm_acc("zb", "ptb", "za")
        yield
        mm_ev("pta", "pb", "ptb", "act")
        yield
        mm_acc("za", "pta", "zb")
        yield

    def chunk_seq(l, it, h, ci):
        m, K, cc, qTc, kTc, vTc, kv, tkk = chunk_ctx(it, h, ci)
        sc = lambda i: tk[:, ci, i, h:h + 1]
        gi = it * 2 + ci
        sin, sout = gi % 2, (gi + 1) % 2
        Zn = "za"
        Rn, Un, O2n, On, ONn = "pb", "ptb", "pa", "mg", "dec"
        Sk_in, Sk_out = ("S", h, sin), ("S", h, sout)
        Sin_ap, Sout_ap = Sst[:, h, sin, :], Sst[:, h, sout, :]
        if gi == 0:
            V("memset", [], [Sk_in], ap=Sin_ap, constant=0.0)
        p7 = psQ()
        MM([kv, Sk_in], p7.k, p7.ap(), kTc, Sin_ap)
        V("scalar_tensor_tensor", p7.k + [tkk, K("vtm")], [K(Rn)], out=m[Rn], in0=p7.ap(), scalar=sc(1),
          in1=m["vtm"], op0=ALU.mult, op1=ALU.add)
        yield
        p8 = psQ()
        MM([K(Zn), K(Rn)], p8.k, p8.ap(), m[Zn], m[Rn])
        A(p8.k + [("bg", ci)], [K(Un)], m[Un], p8.ap(), AF.Copy, scale=bg[:, ci, h:h + 1])
        yield
        p11 = psQ()
        MM([K("ktm"), K(Un)], p11.k, p11.ap(), m["ktm"], m[Un])
        V("scalar_tensor_tensor", [Sk_in, tkk] + p11.k, [Sk_out], out=Sout_ap, in0=Sin_ap, scalar=sc(3), in1=p11.ap(),
          op0=ALU.mult, op1=ALU.add)
        yield
        p10 = psQ()
        MM([K("qkd"), K(Un)], p10.k, p10.ap(), m["qkd"], m[Un])
        A(p10.k, [K(O2n)], m[O2n], p10.ap(), AF.Copy)
        p9 = psQ()
        MM([kv, Sk_in], p9.k, p9.ap(), qTc, Sin_ap)
        V("scalar_tensor_tensor", p9.k + [tkk, K(O2n)], [K(On)], out=m[On], in0=p9.ap(), scalar=sc(0),
          in1=m[O2n], op0=ALU.mult, op1=ALU.add)
        yield
        sk = ("small", 1 + 2 * (h % 2) + ci)
        so = 4 + (2 * (h % 2) + ci) * 3
        V("memset", [], [sk], ap=small[:, so:so + 1], constant=0.0)
        A([K(On)], [K(ONn), sk], m[ONn], m[On], AF.Square, accum_out=small[:, so:so + 1])
        A([sk, "cst"], [sk], small[:, so + 1:so + 2], small[:, so:so + 1], AF.Ln, bias=epsc, scale=1.0 / 128)
        A([sk], [sk], small[:, so + 2:so + 3], small[:, so + 1:so + 2], AF.Exp, scale=-0.5)
        yield
        A([K(On), sk], [K(ONn)], m[ONn], m[On], AF.Copy, scale=small[:, so + 2:so + 3])
        p12 = psQ()
        TR([K(ONn), "cst"], p12.k, p12.ap(), m[ONn], ident)
        V("tensor_tensor", p12.k + [("zg", h % 2)], [("mixT", 2 + h)], out=mixT[:, 2 + h, cc:cc + 128], in0=p12.ap(),
          in1=zgs[h % 2][:, cc:cc + 128], op=ALU.mult)
        yield

    t3 = sS[:, 0:2, :].rearrange("p a (k n) -> p (a k) n", n=NS)
    bcS = sS[:, 8:10, :].rearrange("p a (k n) -> p (a k) n", n=NS)

    def mix_sample(l):
        A(["xs"], ["sq16s"], sq16s, xs[:], AF.Square)
        ps = psH()
        for kc in range(KC):
            MM(["sq16s", "ones16"], ps.k, ps.ap(128, NS), ones16[:], sq16s[:, kc, :], start=(kc == 0), stop=(kc == KC - 1))
        rms_rstd(ps, NS, 1.0 / D, "rstd", rstd[:, 0:NS])
        V("tensor_tensor", ["xs", "rstd"], ["sS0"], out=t3, in0=xs[:], in1=rstd[:, 0:NS].unsqueeze(1).to_broadcast([128, KC, NS]),
          op=ALU.mult)
        V("tensor_tensor", ["sS0", "A1s"], ["sS0"], out=t3, in0=t3, in1=A1s[:], op=ALU.mult)
        V("tensor_tensor", ["sS0", "mod"], [("hs", kc) for kc in range(KC)], out=hs, in0=t3, in1=mod[:, 0:8, 1:17], op=ALU.add)
        psp = psF()
        cols = [i * 128 for i in range(18)] + [2304] + [2312 + i * 128 for i in range(4)]
        for i, col in enumerate(cols):
            mcols = 8 if col == 2304 else 128
            for kc in range(KC):
                MM([win_key(col), ("hs", kc)], psp.k, psp.ap(mcols, NS, i * NS), w_in16[:, kc, col:col + mcols], hs[:, kc, :],
                   start=(kc == 0), stop=(kc == KC - 1))
        V("tensor_copy", psp.k, ["projS"], out=projS.rearrange("p a n -> p (a n)"), in_=psp.ap(128, 23 * NS))

        for c in range(2):
            pk = ("pexS", c)
            dma("sp", [], [pk], pexS[:, c, 0:15, :], st_pool_d[l, c * 128:(c + 1) * 128, :, :])
            V("tensor_copy", ["projS"], [pk], out=pexS[:, c, 15, :], in_=projS[:, c, :])
            dma("sp", [pk], [], o_pool_s[l, c * 128:(c + 1) * 128, :, :], pexS[:, c, 1:16, :])
            for hf_ in range(2):
                w = WINDOWS[2 * c + hf_]
                sl = slice(hf_ * 64, hf_ * 64 + 64)
                V("tensor_reduce", [pk], ["sS2"], out=sS[sl, 2, 0:NS], in_=pexS[sl, c, 16 - w:16, :].rearrange("p r b -> p b r"),
                  axis=AX.X, op=ALU.add)
            V("scalar_tensor_tensor", ["sS2", "cst", "projS"], ["sS3"], out=sS[:, 3, 0:NS], in0=sS[:, 2, 0:NS],
              scalar=cst[:, C_INVW + c:C_INVW + c + 1], in1=projS[:, c, :], op0=ALU.mult, op1=ALU.subtract)
            V("tensor_copy", ["sS3"], ["pd16s"], out=pd16s, in_=sS[:, 3, 0:NS])
            ps2 = psH()
            MM(["pd16s"] + PWK, ps2.k, ps2.ap(128, NS), pwbd16[:, c, :], pd16s)
            A(ps2.k + ["pp"], [("mixS", c)], mixS[:, c, :], ps2.ap(128, NS), AF.Copy, scale=pp[:, P_PSC + c:P_PSC + c + 1])

        for c in range(2):
            ck_ = ("cexS", c)
            dma("sp", [], [ck_], cexS[:, c, 0:30, :], st_conv_d[l, c * 128:(c + 1) * 128, :, :])
            sigm(["projS"], "sS4", sS[:, 4, 0:NS], projS[:, 21 + c, :])
            V("tensor_tensor", ["projS", "sS4"], [ck_], out=cexS[:, c, 30, :], in0=projS[:, 19 + c, :], in1=sS[:, 4, 0:NS], op=ALU.mult)
            dma("sp", [ck_], [], o_conv_s[l, c * 128:(c + 1) * 128, :, :], cexS[:, c, 1:31, :])
            V("tensor_tensor", [ck_, "pp"], ["cprod"], out=cprod, in0=cexS[:, c, :, :],
              in1=pp[:, P_CW + c * 31:P_CW + c * 31 + 31].unsqueeze(2).to_broadcast([128, 31, NS]), op=ALU.mult)
            V("tensor_reduce", ["cprod"], ["sS5"], out=sS[:, 5, 0:NS], in_=cprod.rearrange("p r b -> p b r"), axis=AX.X, op=ALU.add)
            V("tensor_scalar", ["sS5", "pp"], ["sS5"], out=sS[:, 5, 0:NS], in0=sS[:, 5, 0:NS], scalar1=pp[:, P_CB + c:P_CB + c + 1],
              scalar2=None, op0=ALU.add)
            psm = psH()
            MM(["sS5", "cst"], psm.k, psm.ap(128, NS), blk64, sS[:, 5, 0:NS])
            V("tensor_tensor", ["sS5"] + psm.k, ["sS6"], out=sS[:, 6, 0:NS], in0=sS[:, 5, 0:NS], in1=psm.ap(128, NS), op=ALU.subtract)
            A(["sS6"], ["sS5"], sS[:, 5, 0:NS], sS[:, 6, 0:NS], AF.Square)
            psv = psH()
            MM(["sS5", "cst"], psv.k, psv.ap(128, NS), blk64, sS[:, 5, 0:NS])
            rms_rstd(psv, NS, 1.0, "rstd", rstd[:, 0:NS])
            V("tensor_tensor", ["sS6", "rstd"], ["sS6"], out=sS[:, 6, 0:NS], in0=sS[:, 6, 0:NS], in1=rstd[:, 0:NS], op=ALU.mult)
            A(["sS6", "pp"], ["sS5"], sS[:, 5, 0:NS], sS[:, 6, 0:NS], AF.Identity, scale=pp[:, P_LG + c:P_LG + c + 1],
              bias=pp[:, P_LB + c:P_LB + c + 1])
            sigm(["sS5"], "sS6", sS[:, 6, 0:NS], sS[:, 5, 0:NS])
            V("tensor_tensor", ["sS5", "sS6"], [("csc16s", c)], out=csc16s[:, c, :], in0=sS[:, 5, 0:NS], in1=sS[:, 6, 0:NS], op=ALU.mult)
        for oc in range(2):
            ps = psH()
            for c in range(2):
                MM([("csc16s", c), "cpw"], ps.k, ps.ap(128, NS), cpw16[:, c, oc * 128:(oc + 1) * 128], csc16s[:, c, :],
                   start=(c == 0), stop=(c == 1))
            A(ps.k, [("mixS", 6 + oc)], mixS[:, 6 + oc, :], ps.ap(128, NS), AF.Copy)

        dma("sp", [], ["qexS"], qexS[:, :, 0:3, :], st_qkv_d[l].rearrange("(c p) r b -> p c r b", p=128))
        V("tensor_copy", ["projS"], ["qexS"], out=qexS[:, :, 3, :], in_=projS[:, 2:14, :])
        dma("sp", ["qexS"], [], o_qkv_s[l].rearrange("(c p) r b -> p c r b", p=128), qexS[:, :, 1:4, :])
        V("tensor_tensor", ["qexS", "pp"], ["sprod"], out=sprod, in0=qexS,
          in1=pp[:, P_QW:P_QW + 48].rearrange("p (c k) -> p c k", k=4).unsqueeze(3).to_broadcast([128, 12, 4, NS]), op=ALU.mult)
        V("tensor_reduce", ["sprod"], ["qkvS"], out=qkvS, in_=sprod.rearrange("p c r b -> p c b r"), axis=AX.X, op=ALU.add)
        sigm(["qkvS"], "sprod", sprod[:, :, 0, :], qkvS)
        V("tensor_tensor", ["qkvS", "sprod"], ["qkvS"], out=qkvS, in0=qkvS, in1=sprod[:, :, 0, :], op=ALU.mult)
        A(["qkvS"], ["sq16s"], sq16s, qkvS[:, 0:8, :], AF.Square)
        psn = psH()
        for i in range(8):
            MM(["sq16s", "ones16"], psn.k, psn.ap(128, NS, i * NS), ones16[:], sq16s[:, i, :])
        rms_rstd(psn, 8 * NS, 1.0, "rstd", rstd[:, 0:8 * NS])
        V("tensor_tensor", ["qkvS", "rstd"], ["qkvS"], out=qkvS[:, 0:8, :], in0=qkvS[:, 0:8, :],
          in1=rstd[:, 0:8 * NS].rearrange("p (a n) -> p a n", n=NS), op=ALU.mult)
        V("tensor_scalar", ["qkvS"], ["qkvS"], out=qkvS[:, 0:4, :], in0=qkvS[:, 0:4, :], scalar1=128.0 ** -0.5, scalar2=None,
          op0=ALU.mult)
        sigm(["projS"], "browS", browS, projS[0:8, 18, :], p=8)
        A(["projS", "pp"], ["growS"], growS, projS[0:8, 18, :], AF.Exp, bias=pp[0:8, P_DTB:P_DTB + 1], scale=1.0)
        A(["growS", "cst"], ["growS"], growS, growS, AF.Ln, bias=onec[0:8, :], scale=1.0)
        V("tensor_scalar", ["growS", "lay16", "cst"], ["growS"], out=growS, in0=growS, scalar1=lay[0:8, 16:17],
          scalar2=cst[0:8, C_M47:C_M47 + 1], op0=ALU.mult, op1=ALU.mult)
        V("scalar_tensor_tensor", ["browS", "growS", "cst"], ["combS"], out=combS, in0=browS,
          scalar=cst[0:8, C_M03:C_M03 + 1], in1=growS, op0=ALU.mult, op1=ALU.add)
        psbc = psH()
        for r in range(8):
            V("tensor_scalar", ["combS", "cst"], [("selS", r % 2)], out=selS[:, r % 2, :], in0=combS, scalar1=ident[0:8, r:r + 1],
              scalar2=None, op0=ALU.mult)
            MM([("selS", r % 2), "cst"], psbc.k, psbc.ap(128, NS, r * NS), ones32[0:8, :], selS[:, r % 2, :])
        V("tensor_copy", psbc.k, ["bcS"], out=bcS[:, 0:4, :], in_=psbc.ap(128, 4 * NS).rearrange("p (a n) -> p a n", n=NS))
        A(psbc.k, ["bcS"], bcS[:, 4:8, :], psbc.ap(128, 4 * NS, 4 * NS).rearrange("p (a n) -> p a n", n=NS), AF.Exp)
        for h in range(4):
            kT_, qT_, vT_ = qkvS[:, 4 + h, :], qkvS[:, h, :], qkvS[:, 8 + h, :]
            beta_, eg_ = bcS[:, h, :], bcS[:, 4 + h, :]
            dma("sp", [], ["Sin"], Sin, st_S_d[l, :, h].rearrange("b k v -> k b v"))
            pks = psH()
            for b in range(NS):
                MM(["Sin", "qkvS"], pks.k, pks.ap(128, 1, b), Sin[:, b, :], qkvS[:, 4 + h, b:b + 1])
                MM(["Sin", "qkvS"], pks.k, pks.ap(128, 1, NS + b), Sin[:, b, :], qkvS[:, h, b:b + 1])
            V("tensor_tensor", pks.k + ["bcS"], ["sS10"], out=sS[:, 10, 0:NS], in0=pks.ap(128, NS), in1=eg_, op=ALU.mult)
            V("tensor_tensor", ["qkvS", "sS10"], ["sS10"], out=sS[:, 10, 0:NS], in0=vT_, in1=sS[:, 10, 0:NS], op=ALU.subtract)
            V("tensor_tensor", ["sS10", "bcS"], ["sS10"], out=sS[:, 10, 0:NS], in0=sS[:, 10, 0:NS], in1=beta_, op=ALU.mult)
            V("tensor_tensor", ["qkvS"], ["sS11"], out=sS[:, 11, 0:NS], in0=qT_, in1=kT_, op=ALU.mult)
            pqk = psH()
            MM(["sS11", "cst"], pqk.k, pqk.ap(128, NS), ones32, sS[:, 11, 0:NS])
            V("tensor_tensor", pks.k + ["bcS"], ["sS12"], out=sS[:, 12, 0:NS], in0=pks.ap(128, NS, NS), in1=eg_, op=ALU.mult)
            V("tensor_tensor", pqk.k + ["sS10"], ["sS13"], out=sS[:, 13, 0:NS], in0=pqk.ap(128, NS), in1=sS[:, 10, 0:NS], op=ALU.mult)
            V("tensor_tensor", ["sS13", "sS12"], ["sS12"], out=sS[:, 12, 0:NS], in0=sS[:, 12, 0:NS], in1=sS[:, 13, 0:NS], op=ALU.add)
            A(["sS12"], ["sS13"], sS[:, 13, 0:NS], sS[:, 12, 0:NS], AF.Square)
            pss = psH()
            MM(["sS13", "cst"], pss.k, pss.ap(128, NS), ones32, sS[:, 13, 0:NS])
            rms_rstd(pss, NS, 1.0 / 128, "rstd", rstd[:, 0:NS])
            V("tensor_tensor", ["sS12", "rstd"], ["sS12"], out=sS[:, 12, 0:NS], in0=sS[:, 12, 0:NS], in1=rstd[:, 0:NS], op=ALU.mult)
            sigm(["projS"], "sS13", sS[:, 13, 0:NS], projS[:, 14 + h, :])
            V("tensor_tensor", ["sS13", "projS"], ["sS13"], out=sS[:, 13, 0:NS], in0=sS[:, 13, 0:NS], in1=projS[:, 14 + h, :], op=ALU.mult)
            V("scalar_tensor_tensor", ["sS13", "pp", "sS12"], [("mixS", 2 + h)], out=mixS[:, 2 + h, :], in0=sS[:, 13, 0:NS],
              scalar=pp[:, P_DG:P_DG + 1], in1=sS[:, 12, 0:NS], op0=ALU.mult, op1=ALU.mult)
            pt = psQ()
            TR(["qkvS", "cst"], pt.k, pt.ap(NS, 128), kT_, ident)
            V("tensor_copy", pt.k, ["ktmS"], out=ktmS, in_=pt.ap(NS, 128))
            pt2_ = psQ()
            TR(["sS10", "cst"], pt2_.k, pt2_.ap(NS, 128), sS[:, 10, 0:NS], ident)
            V("tensor_copy", pt2_.k, ["utmS"], out=utmS, in_=pt2_.ap(NS, 128))
            for b in range(NS):
                V("tensor_scalar", ["ktmS", "cst"], [("kexp", b % 2)], out=kexp[:, b % 2, :], in0=ktmS, scalar1=ident[0:NS, b:b + 1],
                  scalar2=None, op0=ALU.mult)
                po = psQ()
                MM([("kexp", b % 2), "utmS"], po.k, po.ap(), kexp[:, b % 2, :], utmS)
                V("scalar_tensor_tensor", ["Sin", "bcS"] + po.k, ["Sout"], out=Sout[:, b, :], in0=Sin[:, b, :],
                  scalar=bcS[:, 4 + h, b:b + 1], in1=po.ap(), op0=ALU.mult, op1=ALU.add)
            dma("sp", ["Sout"], [], o_S_s[l, :, h].rearrange("b k v -> k b v"), Sout)

        pso = psH()
        for oc in range(KC):
            for kc in range(KC):
                MM(["wout", ("mixS", kc)], pso.k, pso.ap(128, NS, oc * NS), w_out16[:, kc, oc * 128:(oc + 1) * 128], mixS[:, kc, :],
                   start=(kc == 0), stop=(kc == KC - 1))
        V("tensor_tensor", pso.k + ["mod"], ["sS0"], out=t3, in0=pso.ap(128, KC * NS).rearrange("p (a n) -> p a n", n=NS),
          in1=mod[:, 16:24, 1:17], op=ALU.mult)
        V("tensor_tensor", ["sS0", "xs"], ["xs"], out=xs[:], in0=xs[:], in1=t3, op=ALU.add)

    FT = 512
    tiles = [(i * FT, FT) for i in range(T // FT)] + [(T, NS)]

    def norm_all(out_fn):
        for (c0, n) in tiles[:-1]:
            xkeys = [k for kc in range(KC) for k in xk(kc, c0, n)]
            A(xkeys, ["sqf"], sqf[:, :, 0:n], xT[:, :, c0:c0 + n], AF.Square)
            ps = psF8()
            for kc in range(KC):
                MM(["sqf", "ones16"], ps.k, ps.ap(128, n), ones16[:], sqf[:, kc, 0:n], start=(kc == 0), stop=(kc == KC - 1))
            rms_rstd(ps, n, 1.0 / D, "rstd", rstd[:, 0:n])
            for kc in range(KC):
                out_fn(kc, c0, n)
        A(["xs"], ["sqf"], sqf[:, :, 0:NS], xs[:], AF.Square)
        ps = psF8()
        for kc in range(KC):
            MM(["sqf", "ones16"], ps.k, ps.ap(128, NS), ones16[:], sqf[:, kc, 0:NS], start=(kc == 0), stop=(kc == KC - 1))
        rms_rstd(ps, NS, 1.0 / D, "rstd", rstd[:, 0:NS])
        V("tensor_tensor", ["xs", "rstd"], ["sS0"], out=t3, in0=xs[:], in1=rstd[:, 0:NS].unsqueeze(1).to_broadcast([128, KC, NS]),
          op=ALU.mult)

    def ffn(l):
        def out_fn(kc, c0, n):
            V("scalar_tensor_tensor", xk(kc, c0, n) + ["rstd", "lay"], [("tmpB", kc % 2)], out=tmpB[kc % 2][:, 0:n],
              in0=xT[:, kc, c0:c0 + n], scalar=lay[:, 8 + kc:9 + kc], in1=rstd[:, 0:n], op0=ALU.mult, op1=ALU.mult)
            A([("tmpB", kc % 2), "mod"], [("hf", kc, c0 // FT)], hf[:, kc, c0:c0 + n], tmpB[kc % 2][:, 0:n], AF.Identity,
              bias=mod[:, 24 + kc, 0:1], scale=1.0)
        norm_all(out_fn)
        V("tensor_tensor", ["sS0", "A2s"], ["sS0"], out=t3, in0=t3, in1=A2s[:], op=ALU.mult)
        V("tensor_tensor", ["sS0", "mod"], [("hf", kc, 4) for kc in range(KC)], out=hf[:, :, T:T + NS], in0=t3,
          in1=mod[:, 24:32, 1:17], op=ALU.add)
        NHC = HP // 128
        for hp in range(NHP):
            bi = hp % 2
            dma("pool", [], [("w1p", bi)], w1p[bi], w_ff1_d[l, :, hp * HP:(hp + 1) * HP].rearrange("(k p) n -> p k n", p=128))
            dma("pool", [], [("w2p", bi)], w2p[bi], w_ff2_d[l, hp * HP:(hp + 1) * HP, :].rearrange("(k p) n -> p k n", p=128))
            for hc in range(NHC):
                for ti, (c0, n) in enumerate(tiles):
                    ps = psF8()
                    for kc in range(KC):
                        MM([("w1p", bi), ("hf", kc, ti)], ps.k, ps.ap(128, n), w1p[bi][:, kc, hc * 128:(hc + 1) * 128],
                           hf[:, kc, c0:c0 + n], start=(kc == 0), stop=(kc == KC - 1))
                    tb = (hc * 5 + ti) % 2
                    A(ps.k, [("tmpB", tb)], tmpB[tb][:, 0:n], ps.ap(128, n), AF.Relu)
                    G("tensor_tensor", [("tmpB", tb)], [("apart", bi, hc, ti)], out=apart[bi][:, hc, c0:c0 + n], in0=tmpB[tb][:, 0:n],
                      in1=tmpB[tb][:, 0:n], op=ALU.mult)
            for oc in range(KC):
                for ti, (c0, n) in enumerate(tiles):
                    ps = psF8()
                    for hc in range(NHC):
                        MM([("w2p", bi), ("apart", bi, hc, ti)], ps.k, ps.ap(128, n), w2p[bi][:, hc, oc * 128:(oc + 1) * 128],
                           apart[bi][:, hc, c0:c0 + n], start=(hc == 0), stop=(hc == NHC - 1))
                    if ti < len(tiles) - 1:
                        V("scalar_tensor_tensor", ps.k + xk(oc, c0, n) + ["mod"], xk(oc, c0, n), out=xT[:, oc, c0:c0 + n],
                          in0=ps.ap(128, n), scalar=mod[:, 40 + oc, 0:1], in1=xT[:, oc, c0:c0 + n], op0=ALU.mult, op1=ALU.add)
                    else:
                        V("tensor_tensor", ps.k + ["mod"], ["sS14"], out=sS[:, 14, 0:NS], in0=ps.ap(128, NS), in1=mod[:, 40 + oc, 1:17],
                          op=ALU.mult)
                        V("tensor_tensor", ["sS14", "xs"], ["xs"], out=xs[:, oc, :], in0=xs[:, oc, :], in1=sS[:, 14, 0:NS], op=ALU.add)

    for l in range(L):
        barrier()
        ada_phase(l)
        barrier()
        layer_params(l)
        load_layer_weights(l)
        for it in range(NTILE):
            mix_prompt_tile(l, it)
        out_proj_tile(l, NTILE - 1)
        barrier()
        mix_sample(l)
        barrier()
        ffn(l)
    barrier()

    def fin_fn(kc, c0, n):
        V("scalar_tensor_tensor", xk(kc, c0, n) + ["rstd", "gf"], [("tmpB", kc % 2)], out=tmpB[kc % 2][:, 0:n],
          in0=xT[:, kc, c0:c0 + n], scalar=gf[:, kc:kc + 1], in1=rstd[:, 0:n], op0=ALU.mult, op1=ALU.mult)
        dma("sp", [("tmpB", kc % 2)], [], yT_d[kc * 128:(kc + 1) * 128, c0:c0 + n], tmpB[kc % 2][:, 0:n])
    norm_all(fin_fn)
    V("tensor_tensor", ["sS0", "gf"], ["sS0"], out=t3, in0=t3, in1=gf[:].unsqueeze(2).to_broadcast([128, KC, NS]), op=ALU.mult)
    dma("sp", ["sS0"], [], ysT_d.rearrange("(k p) n -> p k n", p=128), t3)

    print("arena words used", AR.hi, "ops", len(S.ops))
    _HOOK["dbg"] = dict(qkv0=int(qkvb[0].offset), qkv1=int(qkvb[1].offset), qext0=int(qext[0].offset), rstd=0)
    S.analyze()
    S.emit(nc, es)
    es.close()
    return nc


def _consts():
    c = np.zeros((128, NCST), np.float32)
    i = np.arange(128)
    c[:, C_ID:C_ID + 128] = np.eye(128)
    c[:, C_U:C_U + 128] = (i[:, None] <= i[None, :])
    c[:, C_MS:C_MS + 128] = (i[:, None] > i[None, :])
    c[:, C_TS:C_TS + 128] = (i[None, :] > i[:, None])
    c[:, C_ONES:C_ONES + 128] = 1.0
    c[:, C_BLK:C_BLK + 128] = ((i[:, None] // 64) == (i[None, :] // 64)) / 64.0
    for ch in range(2):
        for p in range(128):
            w = WINDOWS[2 * ch + p // 64]
            c[p, C_INVW + ch] = 1.0 / w
            for t in range(15):
                c[p, C_FIX + ch * 15 + t] = w / min(t + 1, w)
    c[:, C_BD:C_BD + 128] = ((i[:, None] // 16) == (i[None, :] // 16))
    c[0:4, C_M03] = 1.0
    c[4:8, C_M47] = 1.0
    c[:, C_EPS] = EPS
    c[:, C_ONE] = 1.0
    return c


_NC_CACHE = {}
_HOOK = {}


def kernel(x_prompt, x_sample, state_pool, state_qkv_conv, state_delta, state_conv, c_prompt, c_sample,
           w_ada, b_ada, g_norm1, g_norm2, w_in, pool_w, pool_scale, qkv_conv_w, a_log, dt_bias,
           dn_norm_g, conf_dw_w, conf_dw_b, conf_ln_g, conf_ln_b, conf_pw_w, w_out, w_ff1, w_ff2, g_final):
    f = lambda a: np.ascontiguousarray(np.asarray(a, dtype=np.float32))
    x_prompt, x_sample = f(x_prompt), f(x_sample)
    pp = np.zeros((128, L, NPAR), np.float32)
    fm = lambda v, n: np.asarray(v, np.float32).reshape(L, n, 128).transpose(2, 0, 1)
    pp[:, :, P_G1:P_G1 + 8] = fm(g_norm1, 8)
    pp[:, :, P_G2:P_G2 + 8] = fm(g_norm2, 8)
    pp[:, :, P_BADA:P_BADA + 48] = fm(b_ada, 48)
    pp[:, :, P_PSC:P_PSC + 2] = fm(pool_scale, 2)
    qw = np.asarray(qkv_conv_w, np.float32).reshape(L, 4, 12, 128).transpose(3, 0, 2, 1)
    pp[:, :, P_QW:P_QW + 48] = qw.reshape(128, L, 48)
    cw = np.asarray(conf_dw_w, np.float32).reshape(L, 31, 2, 128).transpose(3, 0, 2, 1)
    pp[:, :, P_CW:P_CW + 62] = cw.reshape(128, L, 62)
    pp[:, :, P_CB:P_CB + 2] = fm(conf_dw_b, 2)
    pp[:, :, P_LG:P_LG + 2] = fm(conf_ln_g, 2)
    pp[:, :, P_LB:P_LB + 2] = fm(conf_ln_b, 2)
    pp[:, :, P_DG] = np.asarray(dn_norm_g, np.float32).T
    pp[4:8, :, P_ALOG] = np.asarray(a_log, np.float32).T
    pp[4:8, :, P_DTB] = np.asarray(dt_bias, np.float32).T
    gf = np.ascontiguousarray(np.asarray(g_final, np.float32).reshape(8, 128).T)
    cst = _consts()
    shared = dict(pp=pp, gf=gf, cst=cst, w_ada=f(w_ada), w_in=f(w_in), w_out=f(w_out), w_ff1=f(w_ff1), w_ff2=f(w_ff2),
                  pool_w=f(pool_w), conf_pw_w=f(conf_pw_w))
    sp_, sq_, ss_, sc_ = f(state_pool), f(state_qkv_conv), f(state_delta), f(state_conv)
    in_maps = []
    for c in range(NCORES):
        b0, b1 = c * NS, (c + 1) * NS
        m = dict(shared)
        m["xT"] = np.ascontiguousarray(x_prompt[c].T)
        m["xsT"] = np.ascontiguousarray(x_sample[b0:b1, 0, :].T)
        m["cT"] = np.ascontiguousarray(np.concatenate([np.asarray(c_prompt, np.float32)[c:c + 1],
                                                       np.asarray(c_sample, np.float32)[b0:b1]], 0).T)
        m["st_pool"] = np.ascontiguousarray(sp_[:, b0:b1].transpose(0, 3, 2, 1))
        m["st_qkv"] = np.ascontiguousarray(sq_[:, b0:b1].transpose(0, 3, 2, 1))
        m["st_S"] = np.ascontiguousarray(ss_[:, b0:b1])
        m["st_conv"] = np.ascontiguousarray(sc_[:, b0:b1].transpose(0, 3, 2, 1))
        in_maps.append(m)
    if _HOOK.get("prep_only"):
        return in_maps
    if "nc" not in _NC_CACHE:
        _NC_CACHE["nc"] = build_program()
    res = run_bass_kernel_spmd(_NC_CACHE["nc"], in_maps, core_ids=list(range(NCORES)))
    return _post(res.results)


def _post(R):
    cat = lambda fn, ax: np.ascontiguousarray(np.concatenate([fn(r) for r in R], axis=ax))
    y_prompt = cat(lambda r: r["yT"].T[None], 0)
    y_sample = cat(lambda r: r["ysT"].T[:, None, :], 0)
    pool_p = cat(lambda r: r["o_pool_p"].transpose(0, 2, 1)[:, None], 1)
    pool_s = cat(lambda r: r["o_pool_s"].transpose(0, 3, 2, 1), 1)
    qkv_p = cat(lambda r: r["o_qkv_p"].transpose(0, 2, 1)[:, None], 1)
    qkv_s = cat(lambda r: r["o_qkv_s"].transpose(0, 3, 2, 1), 1)
    S_p = cat(lambda r: r["o_S_p"][:, None], 1)
    S_s = cat(lambda r: r["o_S_s"], 1)
    conv_p = cat(lambda r: r["o_conv_p"].transpose(0, 2, 1)[:, None], 1)
    conv_s = cat(lambda r: r["o_conv_s"].transpose(0, 3, 2, 1), 1)
    return tuple(np.asarray(a, np.float32) for a in
                 (y_prompt, y_sample, pool_p, pool_s, qkv_p, qkv_s, S_p, S_s, conv_p, conv_s))
```

```python
import numpy as np
from contextlib import ExitStack
import concourse.bass as bass
import concourse.mybir as mybir
from concourse.bass_utils import run_bass_kernel_spmd

F32 = mybir.dt.float32
BF16 = mybir.dt.bfloat16
AF = mybir.ActivationFunctionType
ALU = mybir.AluOpType
AX = mybir.AxisListType

NCORES = 8
L = 4
D = 1024
KC = 8
T = 2048
NS = 16
TT = 256
NTILE = T // TT
NIN = 2824
DFF = 4096
EPS = 1e-6
FUSE_WAIT = True
USE_LS = True
WINDOWS = (2, 4, 8, 16)

P_G1, P_G2, P_BADA, P_PSC, P_QW, P_CW, P_CB, P_LG, P_LB, P_DG, P_ALOG, P_DTB, NPAR = \
    0, 8, 16, 64, 66, 114, 176, 178, 180, 182, 183, 184, 192
C_ID, C_U, C_MS, C_TS, C_ONES, C_BLK, C_FIX, C_INVW, C_M03, C_M47, C_EPS, C_ONE, C_BD, NCST = \
    0, 128, 256, 384, 512, 640, 768, 798, 800, 801, 802, 803, 804, 932


class _Op:
    __slots__ = ("eng", "call", "reads", "writes", "dma", "waits", "sig", "snap", "signaled", "barrier")


class Sched:
    ENGS = ("pe", "dve", "act", "pool", "sp")
    NDMA = 24

    def __init__(self):
        self.ops = []
        self.efree = {}
        self.kfin = {}
        self.stage_fin = 0.0

    def add(self, eng, meth, reads, writes, args, kw, dma=False):
        o = _Op()
        ex = [k for k in reads if isinstance(k, tuple) and k[0] == "psbank"]
        if ex:
            reads = [k for k in reads if not (isinstance(k, tuple) and k[0] == "psbank")]
            writes = list(writes) + [k for k in ex if k not in writes]
        o.eng, o.call, o.reads, o.writes, o.dma = eng, (meth, args, kw), tuple(reads), tuple(writes), dma
        o.waits, o.sig, o.snap, o.signaled, o.barrier = {}, None, None, False, False
        self.ops.append(o)
        try:
            if dma:
                cost = 2000.0
            else:
                outap = kw.get("out", kw.get("ap", args[0] if args else None))
                n = 1
                for d in outap.shape[1:]:
                    n *= int(d)
                if eng == "pe":
                    if meth == "matmul":
                        cost = (4.0 if kw["lhsT"].dtype == F32 else 1.0) * max(n, 64) / 2.4 + 40.0
                    else:
                        cost = 120.0
                elif eng == "dve":
                    cost = (n + 151) / 0.96
                elif eng == "act":
                    cost = (n + 400) / 1.4
                else:
                    cost = (n + 250) / 0.7
            t0 = self.efree.get(eng, 0.0)
            for k in o.reads + o.writes:
                t = self.kfin.get(k)
                if t is not None and t + 250.0 > t0:
                    t0 = t + 250.0
            t1 = t0 + cost
            self.efree[eng] = t1
            for k in o.writes:
                self.kfin[k] = t1
            self.stage_fin = max(self.stage_fin, t1)
        except Exception:
            pass
        return o

    def barrier(self, dummy_ap):
        o = self.add("dve", "memset", [], [], (), dict(ap=dummy_ap, constant=0.0))
        o.barrier = True

    def analyze(self):
        clock = {e: {} for e in self.ENGS}
        last_w = {}
        readers = {}
        cnt = {e: 0 for e in self.ENGS}
        dma_issued = [0] * self.NDMA
        dma_last = [None] * self.NDMA
        rrq = {}
        last_eng = {}
        pending = {}
        for op in self.ops:
            E = op.eng
            ck = clock[E]
            deps = []
            if op.barrier:
                deps.extend(last_eng.values())
                deps.extend(d for d in dma_last if d is not None)
                pending = {e: op for e in self.ENGS if e != "dve"}
            elif E in pending:
                deps.append(pending.pop(E))
            for r in op.reads:
                p = last_w.get(r)
                if p is not None:
                    deps.append(p)
            for r in op.writes:
                p = last_w.get(r)
                if p is not None:
                    if not (p.eng == E and isinstance(r, tuple) and r[0] == "psbank"):
                        deps.append(p)
                rd = readers.get(r)
                if rd:
                    deps.extend(rd[0].values())
                    deps.extend(rd[1])
            waits = {}
            merges = []
            for p in deps:
                key, val = p.sig
                if key == "pe" and E == "pe" and not op.dma:
                    continue
                if ck.get(key, 0) >= val or waits.get(key, 0) >= val:
                    continue
                waits[key] = val
                p.signaled = True
                merges.append(p.snap)
            if op.dma:
                s = None
                half = self.NDMA // 2
                base = 0 if E == "sp" else half
                r0 = rrq.get(E, 0)
                for i in range(half):
                    c = base + (r0 + i) % half
                    if max(ck.get(("d", c), 0), waits.get(("d", c), 0)) >= dma_issued[c]:
                        s = c
                        break
                if s is None:
                    s = base + r0
                    waits[("d", s)] = dma_issued[s]
                    merges.append(dma_last[s].snap)
                rrq[E] = (s - base + 1) % half
            for k, v in waits.items():
                if ck.get(k, 0) < v:
                    ck[k] = v
            for sn in merges:
                for k, v in sn.items():
                    if ck.get(k, 0) < v:
                        ck[k] = v
            op.waits = waits
            if op.dma:
                dma_issued[s] += 1
                op.sig = (("d", s), dma_issued[s])
                dma_last[s] = op
            else:
                cnt[E] += 1
                op.sig = (E, cnt[E])
            sn = dict(ck)
            sn[op.sig[0]] = op.sig[1]
            op.snap = sn
            if not op.dma:
                last_eng[E] = op
            for r in op.writes:
                last_w[r] = op
                readers[r] = ({}, [])
            for r in op.reads:
                rd = readers.get(r)
                if rd is None:
                    rd = readers[r] = ({}, [])
                if op.dma:
                    rd[1].append(op)
                else:
                    rd[0][E] = op
        self.rank = {}
        c2 = {e: 0 for e in self.ENGS}
        for op in self.ops:
            if not op.dma and op.signaled:
                c2[op.eng] += 1
                self.rank[(op.eng, op.sig[1])] = c2[op.eng]
        self.dma_final = dma_issued
        for op in self.ops:
            op.snap = None

    def emit(self, nc, es):
        sems = {}
        for e in ("pe", "dve", "act", "pool"):
            sems[e] = es.enter_context(nc.semaphore("s_" + e))
        for i in range(self.NDMA):
            sems[("d", i)] = es.enter_context(nc.semaphore("s_d%d" % i))
        per = {e: [o for o in self.ops if o.eng == e] for e in self.ENGS}
        block = es.enter_context(nc.Block())

        def run(E, e):
            for op in per[E]:
                wl = [(sems[key], (16 * val if isinstance(key, tuple) else self.rank[(key, val)])) for key, val in op.waits.items()]
                fused = None
                if wl and not op.dma and FUSE_WAIT:
                    fused = wl.pop()
                for sm, v in wl:
                    e.wait_ge(sm, v)
                meth, args, kw = op.call
                ins = getattr(e, meth)(*args, **kw)
                if fused is not None:
                    ins._wait_ge(fused[0], fused[1])
                if op.dma:
                    ins.then_inc(sems[op.sig[0]], 16)
                elif op.signaled:
                    ins.then_inc(sems[op.sig[0]], 1)
            if E == "sp":
                for i in range(self.NDMA):
                    if self.dma_final[i]:
                        e.wait_ge(sems[("d", i)], 16 * self.dma_final[i])

        @block.tensor
        def _(e):
            run("pe", e)

        @block.vector
        def _(e):
            run("dve", e)

        @block.scalar
        def _(e):
            run("act", e)

        @block.gpsimd
        def _(e):
            run("pool", e)

        @block.sync
        def _(e):
            run("sp", e)


def build_program():
    nc = bass.Bass("TRN2", target_bir_lowering=False)
    S = Sched()
    es = ExitStack()

    def din(name, shape):
        return nc.dram_tensor(name, list(shape), F32, kind="ExternalInput").ap()

    def dout(name, shape):
        return nc.dram_tensor(name, list(shape), F32, kind="ExternalOutput").ap()

    xT_d = din("xT", [D, T])
    xsT_d = din("xsT", [D, NS])
    cT_d = din("cT", [D, 17])
    pp_d = din("pp", [128, L, NPAR])
    gf_d = din("gf", [128, KC])
    cst_d = din("cst", [128, NCST])
    w_ada_d = din("w_ada", [L, D, 6 * D])
    w_in_d = din("w_in", [L, D, NIN])
    w_out_d = din("w_out", [L, D, D])
    w_ff1_d = din("w_ff1", [L, D, DFF])
    w_ff2_d = din("w_ff2", [L, DFF, D])
    pool_w_d = din("pool_w", [L, 4, 64, 64])
    pw_d = din("conf_pw_w", [L, 256, 256])
    st_pool_d = din("st_pool", [L, 256, 15, NS])
    st_qkv_d = din("st_qkv", [L, 1536, 3, NS])
    st_S_d = din("st_S", [L, NS, 4, 128, 128])
    st_conv_d = din("st_conv", [L, 256, 30, NS])

    yT_d = dout("yT", [D, T])
    ysT_d = dout("ysT", [D, NS])
    o_pool_p = dout("o_pool_p", [L, 256, 15])
    o_pool_s = dout("o_pool_s", [L, 256, 15, NS])
    o_qkv_p = dout("o_qkv_p", [L, 1536, 3])
    o_qkv_s = dout("o_qkv_s", [L, 1536, 3, NS])
    o_S_p = dout("o_S_p", [L, 4, 128, 128])
    o_S_s = dout("o_S_s", [L, NS, 4, 128, 128])
    o_conv_p = dout("o_conv_p", [L, 256, 30])
    o_conv_s = dout("o_conv_s", [L, 256, 30, NS])

    def sb(name, shape, dt=F32):
        return es.enter_context(nc.sbuf_tensor("sb_" + name, list(shape), dt))

    def op(eng, meth, r, w, *args, **kw):
        S.add(eng, meth, r, w, args, kw)

    def dma(q, r, w, out, in_):
        S.add(q, "dma_start", r, w, (), dict(out=out, in_=in_), dma=True)

    def V(meth, r, w, **kw):
        op("dve", meth, r, w, **kw)

    def A(r, w, out, in_, func, **kw):
        op("act", "activation", r, w, out=out, in_=in_, func=func, **kw)

    def G(meth, r, w, **kw):
        op("pool", meth, r, w, **kw)

    def MM(r, w, out, lhsT, rhs, start=True, stop=True):
        op("pe", "matmul", r, w, out, lhsT=lhsT, rhs=rhs, start=start, stop=stop)

    def TR(r, w, out, in_, ident):
        op("pe", "transpose", r, w, out, in_, ident)

    psb = [es.enter_context(nc.psum_tensor("ps%d" % i, [128, 512], F32)) for i in range(8)]

    class PS:
        def __init__(self, bank, c0, n):
            self.bank, self.c0, self.n = bank, c0, n
            self.k = [("psbank", bank)]

        def ap(self, p=128, n=None, off=0):
            n = self.n if n is None else n
            return psb[self.bank][0:p, self.c0 + off:self.c0 + off + n]

    rot = {"H": 0, "Q": 0, "F": 0}

    def psH():
        i = rot["H"]
        rot["H"] = (i + 1) % 3
        return PS((0, 1, 7)[i], 0, 256)

    def psQ():
        i = rot["Q"]
        rot["Q"] = (i + 1) % 5
        return PS(2 + i, 0, 128)

    def psF():
        return PS(7, 0, 512)

    rotF8 = [0]

    def psF8():
        i = rotF8[0]
        rotF8[0] = (i + 1) % 8
        return PS(i, 0, 512)

    xT = sb("xT", [128, KC, T])
    xs = sb("xs", [128, KC, NS])
    cst = sb("cst", [128, NCST])
    ones16 = sb("ones16", [128, 128], BF16)
    pp = sb("pp", [128, NPAR])
    gf = sb("gf", [128, KC])
    mod = sb("mod", [128, 48, 17])
    lay = sb("lay", [128, 64])
    A1s = sb("A1s", [128, KC, NS])
    A2s = sb("A2s", [128, KC, NS])
    rstd = sb("rstd", [128, 512])
    tmpA = sb("tmpA", [128, 512])
    tmpB = [sb("tmpB%d" % i, [128, 512]) for i in range(2)]
    comb_ap = tmpB[1][0:8, 0:TT]
    sqv = [tmpB[i][:, :].bitcast(BF16).rearrange("p (k n) -> p k n", k=4) for i in range(2)]
    sS = sb("sS", [128, 15, 64])
    small = sb("small", [128, 16])
    ARENA_W = 31300
    arena = sb("arena", [128, ARENA_W])

    class Arena:
        def __init__(self):
            self.off = 0
            self.hi = 0

        def frame(self, off=0):
            self.off = off

        def alloc(self, shape, dt=F32):
            n = int(np.prod(shape[1:]))
            words = n if dt == F32 else (n + 1) // 2
            a = self.off
            self.off += words
            self.hi = max(self.hi, self.off)
            assert self.off <= ARENA_W, ("arena overflow", self.off)
            v = arena[:, a:a + words]
            if dt != F32:
                v = v.bitcast(dt)[:, 0:n]
            if len(shape) == 3:
                v = v.rearrange("p (a b) -> p a b", a=shape[1])
            elif len(shape) == 4:
                v = v.rearrange("p (a b c) -> p a b c", a=shape[1], b=shape[2])
            if shape[0] < 128:
                v = v[0:shape[0]]
            return v

    AR = Arena()
    w_in16 = AR.alloc([128, KC, NIN], BF16)
    w_out16 = AR.alloc([128, KC, D], BF16)
    pwbd16 = AR.alloc([128, 2, 128], BF16)
    cpw16 = AR.alloc([128, 2, 256], BF16)
    W_END = AR.off
    hT = AR.alloc([128, KC, TT], BF16)
    mixT = AR.alloc([128, KC, TT], BF16)
    pext = AR.alloc([128, 2, 15 + TT])
    ptA = [AR.alloc([128, 15 + TT])] * 2
    ptB = [AR.alloc([128, 15 + TT])] * 2
    pd16_ = AR.alloc([128, TT], BF16)
    pd16 = pd16_.unsqueeze(1).to_broadcast([128, 2, TT])
    gext = AR.alloc([128, 2, 30 + TT])
    cacc = [AR.alloc([128, TT])] * 2
    ccen = [AR.alloc([128, TT])] * 2
    cac2 = [AR.alloc([128, TT])] * 2
    csc16 = AR.alloc([128, 2, TT], BF16)
    qhist = AR.alloc([128, 12, 3])
    qext = [AR.alloc([128, 3, 3 + TT])] * 2
    qkvb = [AR.alloc([128, 3, TT]) for i in range(2)]
    zgs = [AR.alloc([128, TT]) for i in range(2)]
    sqh = [AR.alloc([128, 2, TT], BF16) for i in range(2)]
    brow = tmpA[0:8, 0:TT]
    grow = tmpA[0:8, TT:2 * TT]
    comb = None
    bg = AR.alloc([128, 2, 8])
    tk = AR.alloc([128, 2, 6, 4])
    Sst = AR.alloc([128, 4, 2, 128])
    NSET = 4
    CMN = ("ktm", "vtm", "mg", "dec", "dinc", "qkd", "pa", "pta", "pb", "ptb", "za", "zb")
    cm = [dict((n, AR.alloc([128, 128])) for n in CMN) for i in range(NSET)]
    AR.frame(W_END)
    hs = AR.alloc([128, KC, NS], BF16)
    mixS = AR.alloc([128, KC, NS], BF16)
    sq16s = AR.alloc([128, KC, NS], BF16)
    pd16s = AR.alloc([128, NS], BF16)
    csc16s = AR.alloc([128, 2, NS], BF16)
    projS = AR.alloc([128, 23, NS])
    pexS = AR.alloc([128, 2, 16, NS])
    cexS = AR.alloc([128, 2, 31, NS])
    qexS = AR.alloc([128, 12, 4, NS])
    sprod = AR.alloc([128, 12, 4, NS])
    cprod = AR.alloc([128, 31, NS])
    qkvS = AR.alloc([128, 12, NS])
    browS = AR.alloc([8, NS])
    growS = AR.alloc([8, NS])
    combS = AR.alloc([8, NS])
    selS = AR.alloc([8, 2, NS])
    Sin = AR.alloc([128, NS, 128])
    Sout = AR.alloc([128, NS, 128])
    kexp = AR.alloc([16, 2, 128])
    ktmS = AR.alloc([16, 128])
    utmS = AR.alloc([16, 128])
    AR.frame(0)
    HP = 512
    NHP = DFF // HP
    hf = AR.alloc([128, KC, T + NS], BF16)
    apart = [AR.alloc([128, HP // 128, T + NS], BF16) for i in range(2)]
    w1p = [AR.alloc([128, KC, HP], BF16) for i in range(2)]
    w2p = [AR.alloc([128, HP // 128, D], BF16) for i in range(2)]
    sqf = AR.alloc([128, KC, 512], BF16)
    AR.frame(0)
    c32 = AR.alloc([128, KC, 17])
    csg = AR.alloc([128, KC, 17])
    sc16 = AR.alloc([128, KC, 17], BF16)
    wad = [AR.alloc([128, KC, 1024], BF16) for i in range(2)]

    ident = cst[:, C_ID:C_ID + 128]
    Uinc = cst[:, C_U:C_U + 128]
    Mstrict = cst[:, C_MS:C_MS + 128]
    Tstrict = cst[:, C_TS:C_TS + 128]
    ones32 = cst[:, C_ONES:C_ONES + 128]
    blk64 = cst[:, C_BLK:C_BLK + 128]
    bd16 = cst[:, C_BD:C_BD + 128]
    epsc = cst[:, C_EPS:C_EPS + 1]
    onec = cst[:, C_ONE:C_ONE + 1]

    def xk(kc, c0, n):
        return [("x", kc, t) for t in range(c0 // TT, (c0 + n - 1) // TT + 1)]

    def barrier():
        S.barrier(lay[:, 63:64])

    dma("sp", [], ["cst"], cst[:], cst_d)
    dma("sp", [], ["gf"], gf[:], gf_d)
    for kc in range(KC):
        dma("sp", [], xk(kc, 0, T), xT[:, kc, :], xT_d[kc * 128:(kc + 1) * 128, :])
    dma("sp", [], ["xs"], xs[:], xsT_d.rearrange("(k p) n -> p k n", p=128))
    V("tensor_copy", ["cst"], ["ones16"], out=ones16[:], in_=ones32)

    def ada_phase(l):
        dma("sp", [], ["pp"], pp[:], pp_d[:, l, :])
        dma("sp", [], ["c32"], c32, cT_d.rearrange("(k p) n -> p k n", p=128))
        sigm(["c32"], "csg", csg, c32)
        V("tensor_tensor", ["c32", "csg"], ["sc16"], out=sc16, in0=c32, in1=csg, op=ALU.mult)
        for g in range(6):
            wb = wad[g % 2]
            wk = "wad%d" % (g % 2)
            dma("pool", [], [wk], wb, w_ada_d[l, :, g * 1024:(g + 1) * 1024].rearrange("(k p) n -> p k n", p=128))
            ps = psF()
            for jc in range(8):
                for kc in range(KC):
                    MM([wk, "sc16"], ps.k, ps.ap(128, 17, jc * 17), wb[:, kc, jc * 128:(jc + 1) * 128],
                       sc16[:, kc, :], start=(kc == 0), stop=(kc == KC - 1))
            for jc in range(8):
                j = g * 8 + jc
                if g in (1, 4):
                    V("tensor_scalar", ps.k + ["pp"], ["mod"], out=mod[:, j, :], in0=ps.ap(128, 17, jc * 17),
                      scalar1=pp[:, P_BADA + j:P_BADA + j + 1], scalar2=1.0, op0=ALU.add, op1=ALU.add)
                else:
                    V("tensor_scalar", ps.k + ["pp"], ["mod"], out=mod[:, j, :], in0=ps.ap(128, 17, jc * 17),
                      scalar1=pp[:, P_BADA + j:P_BADA + j + 1], scalar2=None, op0=ALU.add)

    def layer_params(l):
        V("tensor_tensor", ["pp", "mod"], ["lay"], out=lay[:, 0:8], in0=pp[:, P_G1:P_G1 + 8], in1=mod[:, 8:16, 0], op=ALU.mult)
        V("tensor_tensor", ["pp", "mod"], ["lay"], out=lay[:, 8:16], in0=pp[:, P_G2:P_G2 + 8], in1=mod[:, 32:40, 0], op=ALU.mult)
        V("tensor_tensor", ["pp", "mod"], ["A1s"], out=A1s[:], in0=mod[:, 8:16, 1:17],
          in1=pp[:, P_G1:P_G1 + 8].unsqueeze(2).to_broadcast([128, 8, NS]), op=ALU.mult)
        V("tensor_tensor", ["pp", "mod"], ["A2s"], out=A2s[:], in0=mod[:, 32:40, 1:17],
          in1=pp[:, P_G2:P_G2 + 8].unsqueeze(2).to_broadcast([128, 8, NS]), op=ALU.mult)
        A(["pp"], ["lay16"], lay[0:8, 16:17], pp[0:8, P_ALOG:P_ALOG + 1], AF.Exp)
        V("tensor_scalar", ["lay16"], ["lay16"], out=lay[0:8, 16:17], in0=lay[0:8, 16:17], scalar1=-1.0, scalar2=None, op0=ALU.mult)

    WGRP = ((0, 256, "pool"), (256, 768, "q"), (768, 1280, "k"), (1280, 1792, "v"), (1792, 2312, "z"), (2312, 2824, "glu"))

    def load_layer_weights(l):
        for (a, b, key) in WGRP:
            dma("pool", [], [("win", key)], w_in16[:, :, a:b], w_in_d[l, :, a:b].rearrange("(k p) n -> p k n", p=128))
        dma("pool", [], ["wout"], w_out16, w_out_d[l].rearrange("(k p) n -> p k n", p=128))
        dma("pool", [], ["cpw"], cpw16, pw_d[l].rearrange("(k p) n -> p k n", p=128))
        op("pool", "memset", [], ["pwbd16"], pwbd16, 0.0)
        for g in range(4):
            c, hf_ = g // 2, g % 2
            dma("pool", ["pwbd16"], [("pwbd", g)], pwbd16[hf_ * 64:(hf_ + 1) * 64, c, hf_ * 64:(hf_ + 1) * 64], pool_w_d[l, g])

    PWK = ["pwbd16"] + [("pwbd", g) for g in range(4)]

    def win_key(col):
        for (a, b, key) in WGRP:
            if a <= col < b:
                return ("win", key)

    def proj(ps, col, m, rhs_t, rkey, n):
        for kc in range(KC):
            MM([win_key(col), (rkey, kc)], ps.k, ps.ap(m, n), w_in16[:, kc, col:col + m], rhs_t[:, kc, 0:n],
               start=(kc == 0), stop=(kc == KC - 1))

    def rms_rstd(ps, n, scale, w_key, out_ap):
        A(ps.k + ["cst"], ["tmpA"], tmpA[:, 0:n], ps.ap(128, n), AF.Ln, bias=epsc, scale=scale)
        A(["tmpA"], [w_key], out_ap, tmpA[:, 0:n], AF.Exp, scale=-0.5)

    def sigm(r, key, buf, src, p=128):
        A(r, [key], buf, src, AF.Exp, scale=-1.0)
        A([key, "cst"], [key], buf, buf, AF.Ln, bias=onec[0:p, :], scale=1.0)
        A([key], [key], buf, buf, AF.Exp, scale=-1.0)

    def run_ls(gens):
        base = min(S.efree.values()) if S.efree else 0.0
        ready = [base] * len(gens)
        alive = [True] * len(gens)
        blocked = [False] * len(gens)
        spins = 0
        while any(alive):
            i = min((r, j) for j, r in enumerate(ready) if alive[j])[1]
            n0 = len(S.ops)
            S.stage_fin = 0.0
            try:
                next(gens[i])
            except StopIteration:
                alive[i] = False
                continue
            if len(S.ops) == n0:
                blocked[i] = True
                others = [ready[j] for j in range(len(gens)) if alive[j] and j != i and not blocked[j]]
                if not others:
                    for j in range(len(gens)):
                        blocked[j] = False
                    spins += 1
                    assert spins < 1000, "list scheduler deadlock"
                    ready[i] += 1.0
                else:
                    ready[i] = min(others) + 50.0
            else:
                blocked[i] = False
                ready[i] = S.stage_fin
                spins = 0

    def run_rr(gens, weights=None):
        act = [(g, (weights[i] if weights else 1)) for i, g in enumerate(gens)]
        while act:
            for item in list(act):
                g, w = item
                try:
                    for _ in range(w):
                        next(g)
                except StopIteration:
                    act.remove(item)

    def rr_gen(gens):
        act = list(gens)
        while act:
            for g in list(act):
                try:
                    next(g)
                    yield
                except StopIteration:
                    act.remove(g)

    def rms2(ps, n, scale, tmp_ap, tmp_key, out_ap, out_key):
        A(ps.k + ["cst"], [tmp_key], tmp_ap, ps.ap(128, n), AF.Ln, bias=epsc, scale=scale)
        A([tmp_key], [out_key], out_ap, tmp_ap, AF.Exp, scale=-0.5)

    def pool_unit(l, it, c):
        E = 15 + TT
        ek = ("pext", c)
        k1, k2 = "pt1", "pt2"
        p1_, p2_ = ptA[c], ptB[c]
        if it == 0:
            V("memset", [], [ek], ap=pext[:, c, 0:15], constant=0.0)
        else:
            V("tensor_copy", [ek], [ek], out=pext[:, c, 0:15], in_=pext[:, c, TT:TT + 15])
        ps = psH()
        proj(ps, c * 128, 128, hT, "hT", TT)
        A(ps.k, [ek], pext[:, c, 15:E], ps.ap(), AF.Copy)
        yield
        e = pext[:, c, :]
        G("tensor_tensor", [ek], [k1], out=p1_[:, 1:E], in0=e[:, 1:E], in1=e[:, 0:E - 1], op=ALU.add)
        yield
        if c == 0:
            G("tensor_tensor", [k1], [k2], out=p2_[64:128, 3:E], in0=p1_[64:128, 3:E], in1=p1_[64:128, 1:E - 2], op=ALU.add)
        else:
            G("tensor_tensor", [k1], [k2], out=p2_[:, 3:E], in0=p1_[:, 3:E], in1=p1_[:, 1:E - 2], op=ALU.add)
            yield
            G("tensor_tensor", [k2], [k1], out=p1_[:, 7:E], in0=p2_[:, 7:E], in1=p2_[:, 3:E - 4], op=ALU.add)
            yield
            G("tensor_tensor", [k1], [k2], out=p2_[64:128, 15:E], in0=p1_[64:128, 15:E], in1=p1_[64:128, 7:E - 8], op=ALU.add)
        yield
        V("tensor_scalar", [k1, "cst"], [k1], out=p1_[0:64, 15:E], in0=p1_[0:64, 15:E],
          scalar1=cst[0:64, C_INVW + c:C_INVW + c + 1], scalar2=None, op0=ALU.mult)
        V("tensor_scalar", [k2, "cst", k1], [k1], out=p1_[64:128, 15:E], in0=p2_[64:128, 15:E],
          scalar1=cst[64:128, C_INVW + c:C_INVW + c + 1], scalar2=None, op0=ALU.mult)
        yield
        if it == 0:
            G("tensor_tensor", [k1, "cst"], [k1], out=p1_[:, 15:30], in0=p1_[:, 15:30],
              in1=cst[:, C_FIX + c * 15:C_FIX + c * 15 + 15], op=ALU.mult)
        G("tensor_tensor", [k1, ek], ["pd16"], out=pd16[:, c, :], in0=p1_[:, 15:E], in1=pext[:, c, 15:E], op=ALU.subtract)
        yield
        ps2 = psH()
        MM(["pd16"] + PWK, ps2.k, ps2.ap(), pwbd16[:, c, :], pd16[:, c, :])
        A(ps2.k + ["pp"], [("mixT", c)], mixT[:, c, :], ps2.ap(), AF.Copy, scale=pp[:, P_PSC + c:P_PSC + c + 1])
        if it == NTILE - 1:
            dma("sp", [ek], [], o_pool_p[l, c * 128:(c + 1) * 128, :], pext[:, c, TT:TT + 15])
        yield

    def conf_unit(l, it, c):
        ek = ("gext", c)
        ka, kc_, kp = "cacc", "ccen", "cac2"
        acc, cen, ac2 = cacc[c], ccen[c], cac2[c]
        if it == 0:
            V("memset", [], [ek], ap=gext[:, c, 0:30], constant=0.0)
        else:
            V("tensor_copy", [ek], [ek], out=gext[:, c, 0:30], in_=gext[:, c, TT:TT + 30])
        psa = psH()
        proj(psa, 2312 + c * 128, 128, hT, "hT", TT)
        psg = psH()
        proj(psg, 2568 + c * 128, 128, hT, "hT", TT)
        sigm(psg.k, kc_, cen, psg.ap())
        V("tensor_tensor", psa.k + [kc_], [ek], out=gext[:, c, 30:30 + TT], in0=psa.ap(), in1=cen, op=ALU.mult)
        yield
        wc = P_CW + c * 31
        accs = [(acc, ka), (ac2, kp), (cen, kc_)]
        V("tensor_scalar", [ek, "pp"], [ka], out=acc, in0=gext[:, c, 0:TT], scalar1=pp[:, wc:wc + 1],
          scalar2=pp[:, P_CB + c:P_CB + c + 1], op0=ALU.mult, op1=ALU.add)
        V("tensor_scalar", [ek, "pp"], [kp], out=ac2, in0=gext[:, c, 1:1 + TT], scalar1=pp[:, wc + 1:wc + 2], scalar2=None, op0=ALU.mult)
        V("tensor_scalar", [ek, "pp"], [kc_], out=cen, in0=gext[:, c, 2:2 + TT], scalar1=pp[:, wc + 2:wc + 3], scalar2=None, op0=ALU.mult)
        yield
        for k in range(3, 31):
            ab, akey = accs[k % 3]
            V("scalar_tensor_tensor", [ek, "pp", akey], [akey], out=ab, in0=gext[:, c, k:k + TT], scalar=pp[:, wc + k:wc + k + 1],
              in1=ab, op0=ALU.mult, op1=ALU.add)
            if k % 3 == 2:
                yield
        V("tensor_tensor", [ka, kp], [ka], out=acc, in0=acc, in1=ac2, op=ALU.add)
        V("tensor_tensor", [ka, kc_], [ka], out=acc, in0=acc, in1=cen, op=ALU.add)
        yield
        psm = psH()
        MM([ka, "cst"], psm.k, psm.ap(), blk64, acc)
        V("tensor_tensor", [ka] + psm.k, [kc_], out=cen, in0=acc, in1=psm.ap(), op=ALU.subtract)
        yield
        A([kc_], [ka], acc, cen, AF.Square)
        yield
        psv = psH()
        MM([ka, "cst"], psv.k, psv.ap(), blk64, acc)
        rms2(psv, TT, 1.0, acc, ka, acc, ka)
        V("tensor_tensor", [kc_, ka], [kc_], out=cen, in0=cen, in1=acc, op=ALU.mult)
        yield
        A([kc_, "pp"], [ka], acc, cen, AF.Identity, scale=pp[:, P_LG + c:P_LG + c + 1], bias=pp[:, P_LB + c:P_LB + c + 1])
        sigm([ka], kc_, cen, acc)
        V("tensor_tensor", [ka, kc_], [("csc16", c)], out=csc16[:, c, :], in0=acc, in1=cen, op=ALU.mult)
        if it == NTILE - 1:
            dma("sp", [ek], [], o_conv_p[l, c * 128:(c + 1) * 128, :], gext[:, c, TT:TT + 30])
        yield

    def conf_final(l, it):
        for oc in range(2):
            ps = psH()
            for c in range(2):
                MM([("csc16", c), "cpw"], ps.k, ps.ap(), cpw16[:, c, oc * 128:(oc + 1) * 128], csc16[:, c, :],
                   start=(c == 0), stop=(c == 1))
            A(ps.k, [("mixT", 6 + oc)], mixT[:, 6 + oc, :], ps.ap(), AF.Copy)
            yield

    def side_thread(l, it):
        def pools():
            yield from pool_unit(l, it, 0)
            yield from pool_unit(l, it, 1)
        def confs():
            yield from conf_unit(l, it, 0)
            yield from conf_unit(l, it, 1)
        yield from rr_gen([pools(), confs()])
        yield from conf_final(l, it)
        hstate["side_done"] = True
        yield

    def ba_unit(l, it):
        psb_ = psH()
        proj(psb_, 2304, 8, hT, "hT", TT)
        sigm(psb_.k, "tmpA", brow, psb_.ap(8), p=8)
        A(psb_.k + ["pp"], ["tmpA"], grow, psb_.ap(8), AF.Exp, bias=pp[0:8, P_DTB:P_DTB + 1], scale=1.0)
        A(["tmpA", "cst"], ["tmpA"], grow, grow, AF.Ln, bias=onec[0:8, :], scale=1.0)
        V("tensor_scalar", ["tmpA", "lay16", "cst"], ["tmpA"], out=grow, in0=grow, scalar1=lay[0:8, 16:17],
          scalar2=cst[0:8, C_M47:C_M47 + 1], op0=ALU.mult, op1=ALU.mult)
        V("scalar_tensor_tensor", ["tmpA", "tmpA", "cst"], [("tmpB", 1)], out=comb_ap, in0=brow,
          scalar=cst[0:8, C_M03:C_M03 + 1], in1=grow, op0=ALU.mult, op1=ALU.add)
        for ci in range(2):
            pq = psQ()
            TR([("tmpB", 1), "cst"], pq.k, pq.ap(128, 8), comb_ap[:, ci * 128:(ci + 1) * 128], ident[0:8, 0:8])
            V("tensor_copy", pq.k, [("bg", ci)], out=bg[:, ci, :], in_=pq.ap(128, 8))
            pq2 = psQ()
            MM([("bg", ci), "cst"], pq2.k, pq2.ap(128, 8), Uinc, bg[:, ci, :])
            MM([("bg", ci), "cst"], pq2.k, pq2.ap(128, 8, 8), ones32, bg[:, ci, :])
            tkk = ("tk", ci)
            V("tensor_copy", pq2.k, [tkk], out=tk[:, ci, 5, :], in_=pq2.ap(128, 4, 4))
            A(pq2.k, [tkk], tk[:, ci, 0, :], pq2.ap(128, 4, 4), AF.Exp)
            A(pq2.k, [tkk], tk[:, ci, 3, :], pq2.ap(128, 4, 12), AF.Exp)
            V("tensor_scalar", [tkk], [tkk], out=tk[:, ci, 1, :], in0=tk[:, ci, 0, :], scalar1=-1.0, scalar2=None, op0=ALU.mult)
            V("tensor_tensor", pq2.k + [tkk], [("small", 0)], out=small[:, 0:4], in0=pq2.ap(128, 4, 12), in1=tk[:, ci, 5, :],
              op=ALU.subtract)
            A([("small", 0)], [tkk], tk[:, ci, 2, :], small[:, 0:4], AF.Exp)
            V("tensor_scalar", [("bg", ci)], [tkk], out=tk[:, ci, 4, :], in0=bg[:, ci, 0:4], scalar1=-1.0, scalar2=None,
              op0=ALU.mult)

    hstate = {}

    def ba_gen(l, it):
        ba_unit(l, it)
        hstate["ba_done"] = True
        yield

    def head_front(l, it, h):
        b = h % 2
        qx, qv, zgb = qext[b], qkvb[b], zgs[b]
        kx, kv, kz = "qext", ("qkv", b), ("zg", b)
        while hstate.get(("chunks_done", it, h - 2), h < 2) is not True and h >= 2:
            yield
        if it == 0:
            V("memset", [], [("qhist", h)], ap=qhist[:, h * 3:(h + 1) * 3, :], constant=0.0)
        V("tensor_copy", [("qhist", h)], [kx], out=qx[:, :, 0:3], in_=qhist[:, h * 3:(h + 1) * 3, :])
        for j in range(3):
            ps = psH()
            proj(ps, 256 + j * 512 + h * 128, 128, hT, "hT", TT)
            A(ps.k, [kx], qx[:, j, 3:3 + TT], ps.ap(), AF.Copy)
            yield
        V("tensor_copy", [kx], [("qhist", h)], out=qhist[:, h * 3:(h + 1) * 3, :], in_=qx[:, :, TT:TT + 3])
        if it == NTILE - 1:
            for j in range(3):
                dma("sp", [("qhist", h)], [], o_qkv_p[l, j * 512 + h * 128:j * 512 + (h + 1) * 128, :], qhist[:, h * 3 + j, :])
        psz = psH()
        proj(psz, 1792 + h * 128, 128, hT, "hT", TT)
        sigm(psz.k, kz, zgb, psz.ap())
        V("scalar_tensor_tensor", psz.k + [kz, "pp"], [kz], out=zgb, in0=psz.ap(), scalar=pp[:, P_DG:P_DG + 1], in1=zgb,
          op0=ALU.mult, op1=ALU.mult)
        yield
        for k in range(4):
            for j in range(3):
                wq = P_QW + (j * 4 + h) * 4
                kvj = ("qkvj", b, j)
                if k == 0:
                    V("tensor_scalar", [kx, "pp"], [kvj, kv], out=qv[:, j, :], in0=qx[:, j, 0:TT], scalar1=pp[:, wq:wq + 1], scalar2=None,
                      op0=ALU.mult)
                else:
                    V("scalar_tensor_tensor", [kx, "pp", kvj], [kvj] + ([kv] if k == 3 else []), out=qv[:, j, :], in0=qx[:, j, k:k + TT],
                      scalar=pp[:, wq + k:wq + k + 1], in1=qv[:, j, :], op0=ALU.mult, op1=ALU.add)
            yield
        sigm([kv], kx, qx[:, :, 0:TT], qv)
        V("tensor_tensor", [kv, kx], [kv], out=qv, in0=qv, in1=qx[:, :, 0:TT], op=ALU.mult)
        A([kv], [("sqh", b)], sqh[b], qv[:, 0:2, :], AF.Square)
        yield
        psn = psF()
        MM([("sqh", b), "ones16"], psn.k, psn.ap(), ones16[:], sqh[b])
        rms2(psn, 2 * TT, 1.0, tmpA[:, 0:2 * TT], "tmpA", rstd[:, 0:2 * TT], "rstd")
        yield
        V("scalar_tensor_tensor", [kv, "rstd"], [kv], out=qv[:, 0, :], in0=qv[:, 0, :], scalar=128.0 ** -0.5,
          in1=rstd[:, 0:TT], op0=ALU.mult, op1=ALU.mult)
        V("tensor_tensor", [kv, "rstd"], [kv], out=qv[:, 1, :], in0=qv[:, 1, :], in1=rstd[:, TT:2 * TT], op=ALU.mult)
        hstate[("front_done", it, h)] = True
        yield

    def prep_thread(l, it, par):
        for u in range(par, 8, 4):
            h, ci = u // 2, u % 2
            while (hstate.get("ba_done") is not True or hstate.get(("front_done", it, h)) is not True
                   or (u >= 4 and hstate.get(("seq_done", u - 4)) is not True)):
                yield
            yield from chunk_prep(l, it, h, ci)
            hstate[("prep_done", u)] = True
            yield

    def seq_thread(l, it, hp):
        for u in (2 * hp, 2 * hp + 1, 2 * hp + 4, 2 * hp + 5):
            h, ci = u // 2, u % 2
            while hstate.get(("prep_done", u)) is not True:
                yield
            yield from chunk_seq(l, it, h, ci)
            hstate[("seq_done", u)] = True
            if ci == 1:
                if it == NTILE - 1:
                    fin = (NTILE * 2) % 2
                    dma("sp", [("S", h, fin)], [], o_S_p[l, h], Sst[:, h, fin, :])
                hstate[("chunks_done", it, h)] = True
            yield

    def fronts_thread(l, it):
        for h in range(4):
            yield from head_front(l, it, h)


    ln_done = {}

    def ln_ops(l, it):
        c0 = it * TT
        xkeys = [k for kc in range(KC) for k in xk(kc, c0, TT)]
        for i2 in range(2):
            A(xkeys, [("tmpB", i2)], sqv[i2], xT[:, 4 * i2:4 * i2 + 4, c0:c0 + TT], AF.Square)
        yield
        ps = psH()
        for kc in range(KC):
            MM([("tmpB", kc // 4), "ones16"], ps.k, ps.ap(), ones16[:], sqv[kc // 4][:, kc % 4, :], start=(kc == 0), stop=(kc == KC - 1))
        rms_rstd(ps, TT, 1.0 / D, "rstd", rstd[:, 0:TT])
        yield
        for kc in range(KC):
            V("scalar_tensor_tensor", xk(kc, c0, TT) + ["rstd", "lay"], [("tmpB", kc % 2)],
              out=tmpB[kc % 2][:, 0:TT], in0=xT[:, kc, c0:c0 + TT], scalar=lay[:, kc:kc + 1], in1=rstd[:, 0:TT],
              op0=ALU.mult, op1=ALU.mult)
            A([("tmpB", kc % 2), "mod"], [("hT", kc)], hT[:, kc, :], tmpB[kc % 2][:, 0:TT], AF.Identity,
              bias=mod[:, kc, 0:1], scale=1.0)
            yield
        ln_done[(l, it)] = True

    def ln_next(l, it):
        while not (hstate.get("ba_done") is True and hstate.get("side_done") is True
                   and all(hstate.get(("front_done", it, h)) is True for h in range(4))):
            yield
        yield from ln_ops(l, it + 1)

    def mix_prompt_tile(l, it):
        if (l, it) not in ln_done:
            for _ in ln_ops(l, it):
                pass
        if it > 0:
            out_proj_tile(l, it - 1)
        hstate.clear()
        if USE_LS:
            run_ls([ba_gen(l, it), fronts_thread(l, it), prep_thread(l, it, 0), prep_thread(l, it, 1), prep_thread(l, it, 2),
                    prep_thread(l, it, 3), seq_thread(l, it, 0), seq_thread(l, it, 1), side_thread(l, it)]
                   + ([ln_next(l, it)] if it + 1 < NTILE else []))
        else:
            run_rr([ba_gen(l, it), fronts_thread(l, it), prep_thread(l, it, 0), prep_thread(l, it, 1), prep_thread(l, it, 2),
                    prep_thread(l, it, 3), seq_thread(l, it, 0), seq_thread(l, it, 1), side_thread(l, it)],
                   weights=[1, 2, 2, 2, 2, 2, 2, 2, 3])

    def out_proj_tile(l, it):
        c0 = it * TT
        for oc in range(KC):
            ps = psH()
            for kc in range(KC):
                MM(["wout", ("mixT", kc)], ps.k, ps.ap(), w_out16[:, kc, oc * 128:(oc + 1) * 128], mixT[:, kc, :],
                   start=(kc == 0), stop=(kc == KC - 1))
            V("scalar_tensor_tensor", ps.k + xk(oc, c0, TT) + ["mod"], xk(oc, c0, TT), out=xT[:, oc, c0:c0 + TT],
              in0=ps.ap(), scalar=mod[:, 16 + oc, 0:1], in1=xT[:, oc, c0:c0 + TT], op0=ALU.mult, op1=ALU.add)

    def chunk_ctx(it, h, ci):
        b = h % 2
        si = (2 * h + ci) % NSET
        m = cm[si]
        K = lambda n: "cm%d_%s" % (si, n)
        cc = ci * 128
        qv = qkvb[b]
        return m, K, cc, qv[:, 0, cc:cc + 128], qv[:, 1, cc:cc + 128], qv[:, 2, cc:cc + 128], ("qkv", b), ("tk", ci)

    def chunk_prep(l, it, h, ci):
        m, K, cc, qTc, kTc, vTc, kv, tkk = chunk_ctx(it, h, ci)
        sc = lambda i: tk[:, ci, i, h:h + 1]
        p1 = psQ()
        TR([kv, "cst"], p1.k, p1.ap(), kTc, ident)
        A(p1.k + [tkk], [K("ktm")], m["ktm"], p1.ap(), AF.Copy, scale=sc(2))
        p2 = psQ()
        TR([kv, "cst"], p2.k, p2.ap(), vTc, ident)
        V("tensor_copy", p2.k, [K("vtm")], out=m["vtm"], in_=p2.ap())
        yield
        V("tensor_scalar", ["cst", ("bg", ci)], [K("mg")], out=m["mg"], in0=Mstrict, scalar1=bg[:, ci, 4 + h:5 + h],
          scalar2=None, op0=ALU.mult)
        p3 = psQ()
        MM([K("mg"), "cst"], p3.k, p3.ap(), m["mg"], Uinc)
        A(p3.k, [K("dec")], m["dec"], p3.ap(), AF.Exp)
        yield
        G("tensor_tensor", [K("dec"), "cst"], [K("dinc")], out=m["dinc"], in0=m["dec"], in1=Uinc, op=ALU.mult)
        p5 = psQ()
        MM([kv], p5.k, p5.ap(), kTc, qTc)
        V("tensor_tensor", p5.k + [K("dinc")], [K("qkd")], out=m["qkd"], in0=p5.ap(), in1=m["dinc"], op=ALU.mult)
        yield
        G("tensor_tensor", [K("dec"), "cst"], [K("dinc")], out=m["dinc"], in0=m["dec"], in1=Tstrict, op=ALU.mult)
        p4 = psQ()
        MM([kv], p4.k, p4.ap(), kTc, kTc)
        V("scalar_tensor_tensor", p4.k + [K("dinc"), tkk], [K("ptb")], out=m["ptb"], in0=p4.ap(), scalar=sc(4),
          in1=m["dinc"], op0=ALU.mult, op1=ALU.mult)
        yield
        p6 = psQ()
        TR([K("ptb"), "cst"], p6.k, p6.ap(), m["ptb"], ident)
        A(p6.k, [K("pb")], m["pb"], p6.ap(), AF.Copy)
        yield
        G("tensor_tensor", [K("ptb"), "cst"], [K("pa")], out=m["pa"], in0=m["ptb"], in1=bd16, op=ALU.mult)
        G("tensor_tensor", [K("pb"), "cst"], [K("pta")], out=m["pta"], in0=m["pb"], in1=bd16, op=ALU.mult)
        G("tensor_tensor", [K("ptb"), K("pa")], [K("mg")], out=m["mg"], in0=m["ptb"], in1=m["pa"], op=ALU.subtract)
        V("tensor_tensor", [K("pa"), "cst"], [K("za")], out=m["za"], in0=m["pa"], in1=ident, op=ALU.add)
        yield

        def mm_ev(dst, lhs, rhs, eng):
            pq_ = psQ()
            MM([K(lhs), K(rhs)], pq_.k, pq_.ap(), m[lhs], m[rhs])
            if eng == "act":
                A(pq_.k, [K(dst)], m[dst], pq_.ap(), AF.Copy)
            else:
                V("tensor_copy", pq_.k, [K(dst)], out=m[dst], in_=pq_.ap())

        def mm_acc(dst, lhs, rhs):
            pq_ = psQ()
            MM([K(lhs), K(rhs)], pq_.k, pq_.ap(), m[lhs], m[rhs])
            V("tensor_tensor", pq_.k + [K(rhs)], [K(dst)], out=m[dst], in0=pq_.ap(), in1=m[rhs], op=ALU.add)

        mm_ev("ptb", "pa", "pta", "act")
        mm_ev("pb", "pta", "pa", "dve")
        yield
        mm_acc("zb", "ptb", "za")
        mm_ev("pta", "pb", "ptb", "act")
        yield
        mm_ev("pa", "ptb", "pb", "dve")
        mm_acc("za", "pta", "zb")
        yield
        mm_ev("ptb", "pa", "pta", "act")
        yield
        mm_acc("zb", "ptb", "za")
        yield
        pq_ = psQ()
        TR([K("zb"), "cst"], pq_.k, pq_.ap(), m["zb"], ident)
        A(pq_.k, [K("dec")], m["dec"], pq_.ap(), AF.Copy)
        yield
        mm_ev("pa", "dec", "mg", "dve")
        mm_ev("pta", "mg", "dec", "act")
        yield
        mm_acc("za", "pta", "zb")
        mm_ev("ptb", "pa", "pta", "act")
        yield
        mm_ev("pb", "pta", "pa", "dve")
        mm_acc("zb", "ptb", "za")
        yield
        mm_ev("pta", "pb", "ptb", "act")
        yield
        mm_acc("za", "pta", "zb")
        yield

    def chunk_seq(l, it, h, ci):
        m, K, cc, qTc, kTc, vTc, kv, tkk = chunk_ctx(it, h, ci)
        sc = lambda i: tk[:, ci, i, h:h + 1]
        gi = it * 2 + ci
        sin, sout = gi % 2, (gi + 1) % 2
        Zn = "za"
        Rn, Un, O2n, On, ONn = "pb", "ptb", "pa", "mg", "dec"
        Sk_in, Sk_out = ("S", h, sin), ("S", h, sout)
        Sin_ap, Sout_ap = Sst[:, h, sin, :], Sst[:, h, sout, :]
        if gi == 0:
            V("memset", [], [Sk_in], ap=Sin_ap, constant=0.0)
        p7 = psQ()
        MM([kv, Sk_in], p7.k, p7.ap(), kTc, Sin_ap)
        V("scalar_tensor_tensor", p7.k + [tkk, K("vtm")], [K(Rn)], out=m[Rn], in0=p7.ap(), scalar=sc(1),
          in1=m["vtm"], op0=ALU.mult, op1=ALU.add)
        yield
        p8 = psQ()
        MM([K(Zn), K(Rn)], p8.k, p8.ap(), m[Zn], m[Rn])
        A(p8.k + [("bg", ci)], [K(Un)], m[Un], p8.ap(), AF.Copy, scale=bg[:, ci, h:h + 1])
        yield
        p11 = psQ()
        MM([K("ktm"), K(Un)], p11.k, p11.ap(), m["ktm"], m[Un])
        V("scalar_tensor_tensor", [Sk_in, tkk] + p11.k, [Sk_out], out=Sout_ap, in0=Sin_ap, scalar=sc(3), in1=p11.ap(),
          op0=ALU.mult, op1=ALU.add)
        yield
        p10 = psQ()
        MM([K("qkd"), K(Un)], p10.k, p10.ap(), m["qkd"], m[Un])
        A(p10.k, [K(O2n)], m[O2n], p10.ap(), AF.Copy)
        p9 = psQ()
        MM([kv, Sk_in], p9.k, p9.ap(), qTc, Sin_ap)
        V("scalar_tensor_tensor", p9.k + [tkk, K(O2n)], [K(On)], out=m[On], in0=p9.ap(), scalar=sc(0),
          in1=m[O2n], op0=ALU.mult, op1=ALU.add)
        yield
        sk = ("small", 1 + 2 * (h % 2) + ci)
        so = 4 + (2 * (h % 2) + ci) * 3
        V("memset", [], [sk], ap=small[:, so:so + 1], constant=0.0)
        A([K(On)], [K(ONn), sk], m[ONn], m[On], AF.Square, accum_out=small[:, so:so + 1])
        A([sk, "cst"], [sk], small[:, so + 1:so + 2], small[:, so:so + 1], AF.Ln, bias=epsc, scale=1.0 / 128)
        A([sk], [sk], small[:, so + 2:so + 3], small[:, so + 1:so + 2], AF.Exp, scale=-0.5)
        yield
        A([K(On), sk], [K(ONn)], m[ONn], m[On], AF.Copy, scale=small[:, so + 2:so + 3])
        p12 = psQ()
        TR([K(ONn), "cst"], p12.k, p12.ap(), m[ONn], ident)
        V("tensor_tensor", p12.k + [("zg", h % 2)], [("mixT", 2 + h)], out=mixT[:, 2 + h, cc:cc + 128], in0=p12.ap(),
          in1=zgs[h % 2][:, cc:cc + 128], op=ALU.mult)
        yield

    t3 = sS[:, 0:2, :].rearrange("p a (k n) -> p (a k) n", n=NS)
    bcS = sS[:, 8:10, :].rearrange("p a (k n) -> p (a k) n", n=NS)

    def mix_sample(l):
        A(["xs"], ["sq16s"], sq16s, xs[:], AF.Square)
        ps = psH()
        for kc in range(KC):
            MM(["sq16s", "ones16"], ps.k, ps.ap(128, NS), ones16[:], sq16s[:, kc, :], start=(kc == 0), stop=(kc == KC - 1))
        rms_rstd(ps, NS, 1.0 / D, "rstd", rstd[:, 0:NS])
        V("tensor_tensor", ["xs", "rstd"], ["sS0"], out=t3, in0=xs[:], in1=rstd[:, 0:NS].unsqueeze(1).to_broadcast([128, KC, NS]),
          op=ALU.mult)
        V("tensor_tensor", ["sS0", "A1s"], ["sS0"], out=t3, in0=t3, in1=A1s[:], op=ALU.mult)
        V("tensor_tensor", ["sS0", "mod"], [("hs", kc) for kc in range(KC)], out=hs, in0=t3, in1=mod[:, 0:8, 1:17], op=ALU.add)
        psp = psF()
        cols = [i * 128 for i in range(18)] + [2304] + [2312 + i * 128 for i in range(4)]
        for i, col in enumerate(cols):
            mcols = 8 if col == 2304 else 128
            for kc in range(KC):
                MM([win_key(col), ("hs", kc)], psp.k, psp.ap(mcols, NS, i * NS), w_in16[:, kc, col:col + mcols], hs[:, kc, :],
                   start=(kc == 0), stop=(kc == KC - 1))
        V("tensor_copy", psp.k, ["projS"], out=projS.rearrange("p a n -> p (a n)"), in_=psp.ap(128, 23 * NS))

        for c in range(2):
            pk = ("pexS", c)
            dma("sp", [], [pk], pexS[:, c, 0:15, :], st_pool_d[l, c * 128:(c + 1) * 128, :, :])
            V("tensor_copy", ["projS"], [pk], out=pexS[:, c, 15, :], in_=projS[:, c, :])
            dma("sp", [pk], [], o_pool_s[l, c * 128:(c + 1) * 128, :, :], pexS[:, c, 1:16, :])
            for hf_ in range(2):
                w = WINDOWS[2 * c + hf_]
                sl = slice(hf_ * 64, hf_ * 64 + 64)
                V("tensor_reduce", [pk], ["sS2"], out=sS[sl, 2, 0:NS], in_=pexS[sl, c, 16 - w:16, :].rearrange("p r b -> p b r"),
                  axis=AX.X, op=ALU.add)
            V("scalar_tensor_tensor", ["sS2", "cst", "projS"], ["sS3"], out=sS[:, 3, 0:NS], in0=sS[:, 2, 0:NS],
              scalar=cst[:, C_INVW + c:C_INVW + c + 1], in1=projS[:, c, :], op0=ALU.mult, op1=ALU.subtract)
            V("tensor_copy", ["sS3"], ["pd16s"], out=pd16s, in_=sS[:, 3, 0:NS])
            ps2 = psH()
            MM(["pd16s"] + PWK, ps2.k, ps2.ap(128, NS), pwbd16[:, c, :], pd16s)
            A(ps2.k + ["pp"], [("mixS", c)], mixS[:, c, :], ps2.ap(128, NS), AF.Copy, scale=pp[:, P_PSC + c:P_PSC + c + 1])

        for c in range(2):
            ck_ = ("cexS", c)
            dma("sp", [], [ck_], cexS[:, c, 0:30, :], st_conv_d[l, c * 128:(c + 1) * 128, :, :])
            sigm(["projS"], "sS4", sS[:, 4, 0:NS], projS[:, 21 + c, :])
            V("tensor_tensor", ["projS", "sS4"], [ck_], out=cexS[:, c, 30, :], in0=projS[:, 19 + c, :], in1=sS[:, 4, 0:NS], op=ALU.mult)
            dma("sp", [ck_], [], o_conv_s[l, c * 128:(c + 1) * 128, :, :], cexS[:, c, 1:31, :])
            V("tensor_tensor", [ck_, "pp"], ["cprod"], out=cprod, in0=cexS[:, c, :, :],
              in1=pp[:, P_CW + c * 31:P_CW + c * 31 + 31].unsqueeze(2).to_broadcast([128, 31, NS]), op=ALU.mult)
            V("tensor_reduce", ["cprod"], ["sS5"], out=sS[:, 5, 0:NS], in_=cprod.rearrange("p r b -> p b r"), axis=AX.X, op=ALU.add)
            V("tensor_scalar", ["sS5", "pp"], ["sS5"], out=sS[:, 5, 0:NS], in0=sS[:, 5, 0:NS], scalar1=pp[:, P_CB + c:P_CB + c + 1],
              scalar2=None, op0=ALU.add)
            psm = psH()
            MM(["sS5", "cst"], psm.k, psm.ap(128, NS), blk64, sS[:, 5, 0:NS])
            V("tensor_tensor", ["sS5"] + psm.k, ["sS6"], out=sS[:, 6, 0:NS], in0=sS[:, 5, 0:NS], in1=psm.ap(128, NS), op=ALU.subtract)
            A(["sS6"], ["sS5"], sS[:, 5, 0:NS], sS[:, 6, 0:NS], AF.Square)
            psv = psH()
            MM(["sS5", "cst"], psv.k, psv.ap(128, NS), blk64, sS[:, 5, 0:NS])
            rms_rstd(psv, NS, 1.0, "rstd", rstd[:, 0:NS])
            V("tensor_tensor", ["sS6", "rstd"], ["sS6"], out=sS[:, 6, 0:NS], in0=sS[:, 6, 0:NS], in1=rstd[:, 0:NS], op=ALU.mult)
            A(["sS6", "pp"], ["sS5"], sS[:, 5, 0:NS], sS[:, 6, 0:NS], AF.Identity, scale=pp[:, P_LG + c:P_LG + c + 1],
              bias=pp[:, P_LB + c:P_LB + c + 1])
            sigm(["sS5"], "sS6", sS[:, 6, 0:NS], sS[:, 5, 0:NS])
            V("tensor_tensor", ["sS5", "sS6"], [("csc16s", c)], out=csc16s[:, c, :], in0=sS[:, 5, 0:NS], in1=sS[:, 6, 0:NS], op=ALU.mult)
        for oc in range(2):
            ps = psH()
            for c in range(2):
                MM([("csc16s", c), "cpw"], ps.k, ps.ap(128, NS), cpw16[:, c, oc * 128:(oc + 1) * 128], csc16s[:, c, :],
                   start=(c == 0), stop=(c == 1))
            A(ps.k, [("mixS", 6 + oc)], mixS[:, 6 + oc, :], ps.ap(128, NS), AF.Copy)

        dma("sp", [], ["qexS"], qexS[:, :, 0:3, :], st_qkv_d[l].rearrange("(c p) r b -> p c r b", p=128))
        V("tensor_copy", ["projS"], ["qexS"], out=qexS[:, :, 3, :], in_=projS[:, 2:14, :])
        dma("sp", ["qexS"], [], o_qkv_s[l].rearrange("(c p) r b -> p c r b", p=128), qexS[:, :, 1:4, :])
        V("tensor_tensor", ["qexS", "pp"], ["sprod"], out=sprod, in0=qexS,
          in1=pp[:, P_QW:P_QW + 48].rearrange("p (c k) -> p c k", k=4).unsqueeze(3).to_broadcast([128, 12, 4, NS]), op=ALU.mult)
        V("tensor_reduce", ["sprod"], ["qkvS"], out=qkvS, in_=sprod.rearrange("p c r b -> p c b r"), axis=AX.X, op=ALU.add)
        sigm(["qkvS"], "sprod", sprod[:, :, 0, :], qkvS)
        V("tensor_tensor", ["qkvS", "sprod"], ["qkvS"], out=qkvS, in0=qkvS, in1=sprod[:, :, 0, :], op=ALU.mult)
        A(["qkvS"], ["sq16s"], sq16s, qkvS[:, 0:8, :], AF.Square)
        psn = psH()
        for i in range(8):
            MM(["sq16s", "ones16"], psn.k, psn.ap(128, NS, i * NS), ones16[:], sq16s[:, i, :])
        rms_rstd(psn, 8 * NS, 1.0, "rstd", rstd[:, 0:8 * NS])
        V("tensor_tensor", ["qkvS", "rstd"], ["qkvS"], out=qkvS[:, 0:8, :], in0=qkvS[:, 0:8, :],
          in1=rstd[:, 0:8 * NS].rearrange("p (a n) -> p a n", n=NS), op=ALU.mult)
        V("tensor_scalar", ["qkvS"], ["qkvS"], out=qkvS[:, 0:4, :], in0=qkvS[:, 0:4, :], scalar1=128.0 ** -0.5, scalar2=None,
          op0=ALU.mult)
        sigm(["projS"], "browS", browS, projS[0:8, 18, :], p=8)
        A(["projS", "pp"], ["growS"], growS, projS[0:8, 18, :], AF.Exp, bias=pp[0:8, P_DTB:P_DTB + 1], scale=1.0)
        A(["growS", "cst"], ["growS"], growS, growS, AF.Ln, bias=onec[0:8, :], scale=1.0)
        V("tensor_scalar", ["growS", "lay16", "cst"], ["growS"], out=growS, in0=growS, scalar1=lay[0:8, 16:17],
          scalar2=cst[0:8, C_M47:C_M47 + 1], op0=ALU.mult, op1=ALU.mult)
        V("scalar_tensor_tensor", ["browS", "growS", "cst"], ["combS"], out=combS, in0=browS,
          scalar=cst[0:8, C_M03:C_M03 + 1], in1=growS, op0=ALU.mult, op1=ALU.add)
        psbc = psH()
        for r in range(8):
            V("tensor_scalar", ["combS", "cst"], [("selS", r % 2)], out=selS[:, r % 2, :], in0=combS, scalar1=ident[0:8, r:r + 1],
              scalar2=None, op0=ALU.mult)
            MM([("selS", r % 2), "cst"], psbc.k, psbc.ap(128, NS, r * NS), ones32[0:8, :], selS[:, r % 2, :])
        V("tensor_copy", psbc.k, ["bcS"], out=bcS[:, 0:4, :], in_=psbc.ap(128, 4 * NS).rearrange("p (a n) -> p a n", n=NS))
        A(psbc.k, ["bcS"], bcS[:, 4:8, :], psbc.ap(128, 4 * NS, 4 * NS).rearrange("p (a n) -> p a n", n=NS), AF.Exp)
        for h in range(4):
            kT_, qT_, vT_ = qkvS[:, 4 + h, :], qkvS[:, h, :], qkvS[:, 8 + h, :]
            beta_, eg_ = bcS[:, h, :], bcS[:, 4 + h, :]
            dma("sp", [], ["Sin"], Sin, st_S_d[l, :, h].rearrange("b k v -> k b v"))
            pks = psH()
            for b in range(NS):
                MM(["Sin", "qkvS"], pks.k, pks.ap(128, 1, b), Sin[:, b, :], qkvS[:, 4 + h, b:b + 1])
                MM(["Sin", "qkvS"], pks.k, pks.ap(128, 1, NS + b), Sin[:, b, :], qkvS[:, h, b:b + 1])
            V("tensor_tensor", pks.k + ["bcS"], ["sS10"], out=sS[:, 10, 0:NS], in0=pks.ap(128, NS), in1=eg_, op=ALU.mult)
            V("tensor_tensor", ["qkvS", "sS10"], ["sS10"], out=sS[:, 10, 0:NS], in0=vT_, in1=sS[:, 10, 0:NS], op=ALU.subtract)
            V("tensor_tensor", ["sS10", "bcS"], ["sS10"], out=sS[:, 10, 0:NS], in0=sS[:, 10, 0:NS], in1=beta_, op=ALU.mult)
            V("tensor_tensor", ["qkvS"], ["sS11"], out=sS[:, 11, 0:NS], in0=qT_, in1=kT_, op=ALU.mult)
            pqk = psH()
            MM(["sS11", "cst"], pqk.k, pqk.ap(128, NS), ones32, sS[:, 11, 0:NS])
            V("tensor_tensor", pks.k + ["bcS"], ["sS12"], out=sS[:, 12, 0:NS], in0=pks.ap(128, NS, NS), in1=eg_, op=ALU.mult)
            V("tensor_tensor", pqk.k + ["sS10"], ["sS13"], out=sS[:, 13, 0:NS], in0=pqk.ap(128, NS), in1=sS[:, 10, 0:NS], op=ALU.mult)
            V("tensor_tensor", ["sS13", "sS12"], ["sS12"], out=sS[:, 12, 0:NS], in0=sS[:, 12, 0:NS], in1=sS[:, 13, 0:NS], op=ALU.add)
            A(["sS12"], ["sS13"], sS[:, 13, 0:NS], sS[:, 12, 0:NS], AF.Square)
            pss = psH()
            MM(["sS13", "cst"], pss.k, pss.ap(128, NS), ones32, sS[:, 13, 0:NS])
            rms_rstd(pss, NS, 1.0 / 128, "rstd", rstd[:, 0:NS])
            V("tensor_tensor", ["sS12", "rstd"], ["sS12"], out=sS[:, 12, 0:NS], in0=sS[:, 12, 0:NS], in1=rstd[:, 0:NS], op=ALU.mult)
            sigm(["projS"], "sS13", sS[:, 13, 0:NS], projS[:, 14 + h, :])
            V("tensor_tensor", ["sS13", "projS"], ["sS13"], out=sS[:, 13, 0:NS], in0=sS[:, 13, 0:NS], in1=projS[:, 14 + h, :], op=ALU.mult)
            V("scalar_tensor_tensor", ["sS13", "pp", "sS12"], [("mixS", 2 + h)], out=mixS[:, 2 + h, :], in0=sS[:, 13, 0:NS],
              scalar=pp[:, P_DG:P_DG + 1], in1=sS[:, 12, 0:NS], op0=ALU.mult, op1=ALU.mult)
            pt = psQ()
            TR(["qkvS", "cst"], pt.k, pt.ap(NS, 128), kT_, ident)
            V("tensor_copy", pt.k, ["ktmS"], out=ktmS, in_=pt.ap(NS, 128))
            pt2_ = psQ()
            TR(["sS10", "cst"], pt2_.k, pt2_.ap(NS, 128), sS[:, 10, 0:NS], ident)
            V("tensor_copy", pt2_.k, ["utmS"], out=utmS, in_=pt2_.ap(NS, 128))
            for b in range(NS):
                V("tensor_scalar", ["ktmS", "cst"], [("kexp", b % 2)], out=kexp[:, b % 2, :], in0=ktmS, scalar1=ident[0:NS, b:b + 1],
                  scalar2=None, op0=ALU.mult)
                po = psQ()
                MM([("kexp", b % 2), "utmS"], po.k, po.ap(), kexp[:, b % 2, :], utmS)
                V("scalar_tensor_tensor", ["Sin", "bcS"] + po.k, ["Sout"], out=Sout[:, b, :], in0=Sin[:, b, :],
                  scalar=bcS[:, 4 + h, b:b + 1], in1=po.ap(), op0=ALU.mult, op1=ALU.add)
            dma("sp", ["Sout"], [], o_S_s[l, :, h].rearrange("b k v -> k b v"), Sout)

        pso = psH()
        for oc in range(KC):
            for kc in range(KC):
                MM(["wout", ("mixS", kc)], pso.k, pso.ap(128, NS, oc * NS), w_out16[:, kc, oc * 128:(oc + 1) * 128], mixS[:, kc, :],
                   start=(kc == 0), stop=(kc == KC - 1))
        V("tensor_tensor", pso.k + ["mod"], ["sS0"], out=t3, in0=pso.ap(128, KC * NS).rearrange("p (a n) -> p a n", n=NS),
          in1=mod[:, 16:24, 1:17], op=ALU.mult)
        V("tensor_tensor", ["sS0", "xs"], ["xs"], out=xs[:], in0=xs[:], in1=t3, op=ALU.add)

    FT = 512
    tiles = [(i * FT, FT) for i in range(T // FT)] + [(T, NS)]

    def norm_all(out_fn):
        for (c0, n) in tiles[:-1]:
            xkeys = [k for kc in range(KC) for k in xk(kc, c0, n)]
            A(xkeys, ["sqf"], sqf[:, :, 0:n], xT[:, :, c0:c0 + n], AF.Square)
            ps = psF8()
            for kc in range(KC):
                MM(["sqf", "ones16"], ps.k, ps.ap(128, n), ones16[:], sqf[:, kc, 0:n], start=(kc == 0), stop=(kc == KC - 1))
            rms_rstd(ps, n, 1.0 / D, "rstd", rstd[:, 0:n])
            for kc in range(KC):
                out_fn(kc, c0, n)
        A(["xs"], ["sqf"], sqf[:, :, 0:NS], xs[:], AF.Square)
        ps = psF8()
        for kc in range(KC):
            MM(["sqf", "ones16"], ps.k, ps.ap(128, NS), ones16[:], sqf[:, kc, 0:NS], start=(kc == 0), stop=(kc == KC - 1))
        rms_rstd(ps, NS, 1.0 / D, "rstd", rstd[:, 0:NS])
        V("tensor_tensor", ["xs", "rstd"], ["sS0"], out=t3, in0=xs[:], in1=rstd[:, 0:NS].unsqueeze(1).to_broadcast([128, KC, NS]),
          op=ALU.mult)

    def ffn(l):
        def out_fn(kc, c0, n):
            V("scalar_tensor_tensor", xk(kc, c0, n) + ["rstd", "lay"], [("tmpB", kc % 2)], out=tmpB[kc % 2][:, 0:n],
              in0=xT[:, kc, c0:c0 + n], scalar=lay[:, 8 + kc:9 + kc], in1=rstd[:, 0:n], op0=ALU.mult, op1=ALU.mult)
            A([("tmpB", kc % 2), "mod"], [("hf", kc, c0 // FT)], hf[:, kc, c0:c0 + n], tmpB[kc % 2][:, 0:n], AF.Identity,
              bias=mod[:, 24 + kc, 0:1], scale=1.0)
        norm_all(out_fn)
        V("tensor_tensor", ["sS0", "A2s"], ["sS0"], out=t3, in0=t3, in1=A2s[:], op=ALU.mult)
        V("tensor_tensor", ["sS0", "mod"], [("hf", kc, 4) for kc in range(KC)], out=hf[:, :, T:T + NS], in0=t3,
          in1=mod[:, 24:32, 1:17], op=ALU.add)
        NHC = HP // 128
        for hp in range(NHP):
            bi = hp % 2
            dma("pool", [], [("w1p", bi)], w1p[bi], w_ff1_d[l, :, hp * HP:(hp + 1) * HP].rearrange("(k p) n -> p k n", p=128))
            dma("pool", [], [("w2p", bi)], w2p[bi], w_ff2_d[l, hp * HP:(hp + 1) * HP, :].rearrange("(k p) n -> p k n", p=128))
            for hc in range(NHC):
                for ti, (c0, n) in enumerate(tiles):
                    ps = psF8()
                    for kc in range(KC):
                        MM([("w1p", bi), ("hf", kc, ti)], ps.k, ps.ap(128, n), w1p[bi][:, kc, hc * 128:(hc + 1) * 128],
                           hf[:, kc, c0:c0 + n], start=(kc == 0), stop=(kc == KC - 1))
                    tb = (hc * 5 + ti) % 2
                    A(ps.k, [("tmpB", tb)], tmpB[tb][:, 0:n], ps.ap(128, n), AF.Relu)
                    G("tensor_tensor", [("tmpB", tb)], [("apart", bi, hc, ti)], out=apart[bi][:, hc, c0:c0 + n], in0=tmpB[tb][:, 0:n],
                      in1=tmpB[tb][:, 0:n], op=ALU.mult)
            for oc in range(KC):
                for ti, (c0, n) in enumerate(tiles):
                    ps = psF8()
                    for hc in range(NHC):
                        MM([("w2p", bi), ("apart", bi, hc, ti)], ps.k, ps.ap(128, n), w2p[bi][:, hc, oc * 128:(oc + 1) * 128],
                           apart[bi][:, hc, c0:c0 + n], start=(hc == 0), stop=(hc == NHC - 1))
                    if ti < len(tiles) - 1:
                        V("scalar_tensor_tensor", ps.k + xk(oc, c0, n) + ["mod"], xk(oc, c0, n), out=xT[:, oc, c0:c0 + n],
                          in0=ps.ap(128, n), scalar=mod[:, 40 + oc, 0:1], in1=xT[:, oc, c0:c0 + n], op0=ALU.mult, op1=ALU.add)
                    else:
                        V("tensor_tensor", ps.k + ["mod"], ["sS14"], out=sS[:, 14, 0:NS], in0=ps.ap(128, NS), in1=mod[:, 40 + oc, 1:17],
                          op=ALU.mult)
                        V("tensor_tensor", ["sS14", "xs"], ["xs"], out=xs[:, oc, :], in0=xs[:, oc, :], in1=sS[:, 14, 0:NS], op=ALU.add)

    for l in range(L):
        barrier()
        ada_phase(l)
        barrier()
        layer_params(l)
        load_layer_weights(l)
        for it in range(NTILE):
            mix_prompt_tile(l, it)
        out_proj_tile(l, NTILE - 1)
        barrier()
        mix_sample(l)
        barrier()
        ffn(l)
    barrier()

    def fin_fn(kc, c0, n):
        V("scalar_tensor_tensor", xk(kc, c0, n) + ["rstd", "gf"], [("tmpB", kc % 2)], out=tmpB[kc % 2][:, 0:n],
          in0=xT[:, kc, c0:c0 + n], scalar=gf[:, kc:kc + 1], in1=rstd[:, 0:n], op0=ALU.mult, op1=ALU.mult)
        dma("sp", [("tmpB", kc % 2)], [], yT_d[kc * 128:(kc + 1) * 128, c0:c0 + n], tmpB[kc % 2][:, 0:n])
    norm_all(fin_fn)
    V("tensor_tensor", ["sS0", "gf"], ["sS0"], out=t3, in0=t3, in1=gf[:].unsqueeze(2).to_broadcast([128, KC, NS]), op=ALU.mult)
    dma("sp", ["sS0"], [], ysT_d.rearrange("(k p) n -> p k n", p=128), t3)

    print("arena words used", AR.hi, "ops", len(S.ops))
    _HOOK["dbg"] = dict(qkv0=int(qkvb[0].offset), qkv1=int(qkvb[1].offset), qext0=int(qext[0].offset), rstd=0)
    S.analyze()
    S.emit(nc, es)
    es.close()
    return nc


def _consts():
    c = np.zeros((128, NCST), np.float32)
    i = np.arange(128)
    c[:, C_ID:C_ID + 128] = np.eye(128)
    c[:, C_U:C_U + 128] = (i[:, None] <= i[None, :])
    c[:, C_MS:C_MS + 128] = (i[:, None] > i[None, :])
    c[:, C_TS:C_TS + 128] = (i[None, :] > i[:, None])
    c[:, C_ONES:C_ONES + 128] = 1.0
    c[:, C_BLK:C_BLK + 128] = ((i[:, None] // 64) == (i[None, :] // 64)) / 64.0
    for ch in range(2):
        for p in range(128):
            w = WINDOWS[2 * ch + p // 64]
            c[p, C_INVW + ch] = 1.0 / w
            for t in range(15):
                c[p, C_FIX + ch * 15 + t] = w / min(t + 1, w)
    c[:, C_BD:C_BD + 128] = ((i[:, None] // 16) == (i[None, :] // 16))
    c[0:4, C_M03] = 1.0
    c[4:8, C_M47] = 1.0
    c[:, C_EPS] = EPS
    c[:, C_ONE] = 1.0
    return c


_NC_CACHE = {}
_HOOK = {}


def kernel(x_prompt, x_sample, state_pool, state_qkv_conv, state_delta, state_conv, c_prompt, c_sample,
           w_ada, b_ada, g_norm1, g_norm2, w_in, pool_w, pool_scale, qkv_conv_w, a_log, dt_bias,
           dn_norm_g, conf_dw_w, conf_dw_b, conf_ln_g, conf_ln_b, conf_pw_w, w_out, w_ff1, w_ff2, g_final):
    f = lambda a: np.ascontiguousarray(np.asarray(a, dtype=np.float32))
    x_prompt, x_sample = f(x_prompt), f(x_sample)
    pp = np.zeros((128, L, NPAR), np.float32)
    fm = lambda v, n: np.asarray(v, np.float32).reshape(L, n, 128).transpose(2, 0, 1)
    pp[:, :, P_G1:P_G1 + 8] = fm(g_norm1, 8)
    pp[:, :, P_G2:P_G2 + 8] = fm(g_norm2, 8)
    pp[:, :, P_BADA:P_BADA + 48] = fm(b_ada, 48)
    pp[:, :, P_PSC:P_PSC + 2] = fm(pool_scale, 2)
    qw = np.asarray(qkv_conv_w, np.float32).reshape(L, 4, 12, 128).transpose(3, 0, 2, 1)
    pp[:, :, P_QW:P_QW + 48] = qw.reshape(128, L, 48)
    cw = np.asarray(conf_dw_w, np.float32).reshape(L, 31, 2, 128).transpose(3, 0, 2, 1)
    pp[:, :, P_CW:P_CW + 62] = cw.reshape(128, L, 62)
    pp[:, :, P_CB:P_CB + 2] = fm(conf_dw_b, 2)
    pp[:, :, P_LG:P_LG + 2] = fm(conf_ln_g, 2)
    pp[:, :, P_LB:P_LB + 2] = fm(conf_ln_b, 2)
    pp[:, :, P_DG] = np.asarray(dn_norm_g, np.float32).T
    pp[4:8, :, P_ALOG] = np.asarray(a_log, np.float32).T
    pp[4:8, :, P_DTB] = np.asarray(dt_bias, np.float32).T
    gf = np.ascontiguousarray(np.asarray(g_final, np.float32).reshape(8, 128).T)
    cst = _consts()
    shared = dict(pp=pp, gf=gf, cst=cst, w_ada=f(w_ada), w_in=f(w_in), w_out=f(w_out), w_ff1=f(w_ff1), w_ff2=f(w_ff2),
                  pool_w=f(pool_w), conf_pw_w=f(conf_pw_w))
    sp_, sq_, ss_, sc_ = f(state_pool), f(state_qkv_conv), f(state_delta), f(state_conv)
    in_maps = []
    for c in range(NCORES):
        b0, b1 = c * NS, (c + 1) * NS
        m = dict(shared)
        m["xT"] = np.ascontiguousarray(x_prompt[c].T)
        m["xsT"] = np.ascontiguousarray(x_sample[b0:b1, 0, :].T)
        m["cT"] = np.ascontiguousarray(np.concatenate([np.asarray(c_prompt, np.float32)[c:c + 1],
                                                       np.asarray(c_sample, np.float32)[b0:b1]], 0).T)
        m["st_pool"] = np.ascontiguousarray(sp_[:, b0:b1].transpose(0, 3, 2, 1))
        m["st_qkv"] = np.ascontiguousarray(sq_[:, b0:b1].transpose(0, 3, 2, 1))
        m["st_S"] = np.ascontiguousarray(ss_[:, b0:b1])
        m["st_conv"] = np.ascontiguousarray(sc_[:, b0:b1].transpose(0, 3, 2, 1))
        in_maps.append(m)
    if _HOOK.get("prep_only"):
        return in_maps
    if "nc" not in _NC_CACHE:
        _NC_CACHE["nc"] = build_program()
    res = run_bass_kernel_spmd(_NC_CACHE["nc"], in_maps, core_ids=list(range(NCORES)))
    return _post(res.results)


def _post(R):
    cat = lambda fn, ax: np.ascontiguousarray(np.concatenate([fn(r) for r in R], axis=ax))
    y_prompt = cat(lambda r: r["yT"].T[None], 0)
    y_sample = cat(lambda r: r["ysT"].T[:, None, :], 0)
    pool_p = cat(lambda r: r["o_pool_p"].transpose(0, 2, 1)[:, None], 1)
    pool_s = cat(lambda r: r["o_pool_s"].transpose(0, 3, 2, 1), 1)
    qkv_p = cat(lambda r: r["o_qkv_p"].transpose(0, 2, 1)[:, None], 1)
    qkv_s = cat(lambda r: r["o_qkv_s"].transpose(0, 3, 2, 1), 1)
    S_p = cat(lambda r: r["o_S_p"][:, None], 1)
    S_s = cat(lambda r: r["o_S_s"], 1)
    conv_p = cat(lambda r: r["o_conv_p"].transpose(0, 2, 1)[:, None], 1)
    conv_s = cat(lambda r: r["o_conv_s"].transpose(0, 3, 2, 1), 1)
    return tuple(np.asarray(a, np.float32) for a in
                 (y_prompt, y_sample, pool_p, pool_s, qkv_p, qkv_s, S_p, S_s, conv_p, conv_s))
```
